# Optimizing a Trainium2 kernel written in Bass

```python
import math
import jax, jax.numpy as jnp
from jax import lax
import numpy as np

D_MODEL = 1024
BATCH = 8
SEQ = 8192
DEPTH = 4

N_A = DEPTH // 2
N_B = DEPTH - N_A
ALPHA = (2.0 * DEPTH) ** 0.25
BETA_INIT = (8.0 * DEPTH) ** -0.25
LN_EPS = 1e-5
D_FF = 2816
GDN_HEAD_DIM = 128
GDN_HEADS = D_MODEL // GDN_HEAD_DIM
GDN_WIDTH = GDN_HEADS * GDN_HEAD_DIM
GDN_CONV = 4
GDN_CHUNK = 64
GDN_IN = 4 * GDN_WIDTH + 2 * GDN_HEADS
DIFF_HEAD_DIM = 64
DIFF_HEADS = D_MODEL // (2 * DIFF_HEAD_DIM)
DIFF_WIDTH = DIFF_HEADS * 2 * DIFF_HEAD_DIM
Q_BLOCK = 128
NORM_EPS = 1e-5

kernel_name = "yoco_gdn_diffattn_macaron_deepnorm"


def layer_norm(x, g, b):
    xf = x.astype(jnp.float32)
    mu = jnp.mean(xf, axis=-1, keepdims=True)
    xc = xf - mu
    var = jnp.mean(xc * xc, axis=-1, keepdims=True)
    return (xc * lax.rsqrt(var + LN_EPS) * g + b).astype(x.dtype)


def rms_norm(x, g, eps):
    xf = x.astype(jnp.float32)
    return xf * lax.rsqrt(jnp.mean(xf * xf, axis=-1, keepdims=True) + eps) * g


def l2_norm(x):
    return x * lax.rsqrt(jnp.sum(x * x, axis=-1, keepdims=True) + 1e-6)


def swiglu(x, w_in, w_out):
    gate, up = jnp.split(x @ w_in, 2, axis=-1)
    return (jax.nn.silu(gate) * up) @ w_out


def causal_depthwise_conv(x, w):
    return lax.conv_general_dilated(
        x, w[:, None, :], window_strides=(1,), padding=[(w.shape[0] - 1, 0)],
        dimension_numbers=("NWC", "WIO", "NWC"), feature_group_count=x.shape[-1])


def gated_delta_rule_chunked(q, k, v, g, beta):
    Bsz, T, H, DK = q.shape
    DV = v.shape[-1]
    N = T // GDN_CHUNK

    def to_chunks(t):
        return t.reshape(Bsz, N, GDN_CHUNK, H, -1).transpose(0, 3, 1, 2, 4)

    q, k, v = to_chunks(q), to_chunks(k), to_chunks(v)
    g = g.reshape(Bsz, N, GDN_CHUNK, H).transpose(0, 3, 1, 2)
    beta = beta.reshape(Bsz, N, GDN_CHUNK, H).transpose(0, 3, 1, 2)
    gc = jnp.cumsum(g, axis=-1)
    causal = jnp.tril(jnp.ones((GDN_CHUNK, GDN_CHUNK), dtype=bool))
    strict = jnp.tril(jnp.ones((GDN_CHUNK, GDN_CHUNK), dtype=bool), k=-1)
    decay_mat = jnp.exp(jnp.where(causal, gc[..., :, None] - gc[..., None, :], -jnp.inf))
    kb = k * beta[..., None]
    a_mat = jnp.where(strict, jnp.einsum("bhncd,bhnsd->bhncs", kb, k) * decay_mat, 0.0)
    eye = jnp.eye(GDN_CHUNK, dtype=jnp.float32)
    rhs = jnp.concatenate([kb * jnp.exp(gc)[..., None], v * beta[..., None]], axis=-1)
    wu = lax.linalg.triangular_solve(eye + a_mat, rhs, left_side=True, lower=True, unit_diagonal=True)
    w, u = wu[..., :DK], wu[..., DK:]
    attn = jnp.einsum("bhncd,bhnsd->bhncs", q, k) * decay_mat
    q_dec = q * jnp.exp(gc)[..., None]
    gl = gc[..., -1:]
    k_dec = k * jnp.exp(gl - gc)[..., None]
    chunk_decay = jnp.exp(gl[..., 0])

    def step(S, inp):
        w_n, u_n, qd_n, kd_n, at_n, cd_n = inp
        v_new = u_n - jnp.einsum("bhck,bhkv->bhcv", w_n, S)
        o = jnp.einsum("bhck,bhkv->bhcv", qd_n, S) + jnp.einsum("bhcs,bhsv->bhcv", at_n, v_new)
        S = S * cd_n[..., None, None] + jnp.einsum("bhck,bhcv->bhkv", kd_n, v_new)
        return S, o

    xs = tuple(jnp.moveaxis(t, 2, 0) for t in (w, u, q_dec, k_dec, attn, chunk_decay))
    S0 = jnp.zeros((Bsz, H, DK, DV), jnp.float32)
    _, o = lax.scan(step, S0, xs)
    return o.transpose(1, 0, 3, 2, 4).reshape(Bsz, T, H, DV)


def gated_deltanet(x, w_in, conv_w, a_log, dt_bias, norm_g, w_out):
    Bsz, T, _ = x.shape
    proj = x @ w_in
    qkv = jax.nn.silu(causal_depthwise_conv(proj[..., :3 * GDN_WIDTH], conv_w))
    z = proj[..., 3 * GDN_WIDTH:4 * GDN_WIDTH]
    b = proj[..., 4 * GDN_WIDTH:4 * GDN_WIDTH + GDN_HEADS]
    a = proj[..., 4 * GDN_WIDTH + GDN_HEADS:]
    q, k, v = jnp.split(qkv.astype(jnp.float32), 3, axis=-1)
    shp = (Bsz, T, GDN_HEADS, GDN_HEAD_DIM)
    q = l2_norm(q.reshape(shp)) * (GDN_HEAD_DIM ** -0.5)
    k = l2_norm(k.reshape(shp))
    v = v.reshape(shp)
    beta = jax.nn.sigmoid(b.astype(jnp.float32))
    g = -jnp.exp(a_log.astype(jnp.float32)) * jax.nn.softplus(a.astype(jnp.float32) + dt_bias.astype(jnp.float32))
    o = gated_delta_rule_chunked(q, k, v, g, beta)
    o = rms_norm(o, norm_g.astype(jnp.float32), NORM_EPS) * jax.nn.silu(z.astype(jnp.float32).reshape(shp))
    return o.reshape(Bsz, T, GDN_WIDTH).astype(x.dtype) @ w_out


def diff_softmax_attention(q, k, v, lam):
    Bsz, T, H = q.shape[:3]
    nb = T // Q_BLOCK
    qb = q.reshape(Bsz, nb, Q_BLOCK, H, 2, DIFF_HEAD_DIM).transpose(1, 0, 2, 3, 4, 5)
    kpos = jnp.arange(T)

    def one_block(args):
        q_blk, start = args
        s = jnp.einsum("bqhcd,bkhcd->bhcqk", q_blk, k, preferred_element_type=jnp.float32)
        qpos = start + jnp.arange(Q_BLOCK)
        s = jnp.where((kpos[None, :] <= qpos[:, None])[None, None, None], s, -jnp.inf)
        p = jax.nn.softmax(s, axis=-1)
        att = p[:, :, 0] - lam * p[:, :, 1]
        return jnp.einsum("bhqk,bkhv->bqhv", att.astype(v.dtype), v, preferred_element_type=jnp.float32)

    o = lax.map(one_block, (qb, jnp.arange(nb) * Q_BLOCK))
    return o.transpose(1, 0, 2, 3, 4).reshape(Bsz, T, H, 2 * DIFF_HEAD_DIM)


def diff_attention_layer(x, k_sh, v_sh, w_q, lambda_q, lambda_k, norm_g, w_out, lambda_init):
    Bsz, T, _ = x.shape
    q = (x @ w_q).reshape(Bsz, T, DIFF_HEADS, 2, DIFF_HEAD_DIM) * (DIFF_HEAD_DIM ** -0.5)
    lq = lambda_q.astype(jnp.float32)
    lk = lambda_k.astype(jnp.float32)
    lam = jnp.exp(jnp.sum(lq[0] * lk[0])) - jnp.exp(jnp.sum(lq[1] * lk[1])) + lambda_init
    o = diff_softmax_attention(q, k_sh, v_sh, lam)
    o = rms_norm(o, norm_g.astype(jnp.float32), NORM_EPS) * (1.0 - lambda_init)
    return o.reshape(Bsz, T, DIFF_WIDTH).astype(x.dtype) @ w_out


def setup_inputs(seed: int = 0) -> dict:
    key = jax.random.key(seed)
    ks = jax.random.split(key, 20)
    nrm = jax.random.normal
    f32 = jnp.float32
    x = nrm(ks[0], (BATCH, SEQ, D_MODEL), f32)
    ln_g = 1.0 + 0.02 * nrm(ks[1], (DEPTH, 3, D_MODEL), f32)
    ln_b = 0.02 * nrm(ks[2], (DEPTH, 3, D_MODEL), f32)
    ffn1_w_in = nrm(ks[3], (DEPTH, D_MODEL, 2 * D_FF), f32) * D_MODEL ** -0.5
    ffn1_w_out = nrm(ks[4], (DEPTH, D_FF, D_MODEL), f32) * (D_FF ** -0.5 * BETA_INIT)
    ffn2_w_in = nrm(ks[5], (DEPTH, D_MODEL, 2 * D_FF), f32) * D_MODEL ** -0.5
    ffn2_w_out = nrm(ks[6], (DEPTH, D_FF, D_MODEL), f32) * (D_FF ** -0.5 * BETA_INIT)
    gdn_w_in = nrm(ks[7], (N_A, D_MODEL, GDN_IN), f32) * D_MODEL ** -0.5
    gdn_conv_w = nrm(ks[8], (N_A, GDN_CONV, 3 * GDN_WIDTH), f32) * GDN_CONV ** -0.5
    gdn_a_log = jnp.log(jax.random.uniform(ks[9], (N_A, GDN_HEADS), f32, 1.0, 16.0))
    dt = jnp.exp(jax.random.uniform(ks[10], (N_A, GDN_HEADS), f32) * (math.log(0.1) - math.log(0.001)) + math.log(0.001))
    gdn_dt_bias = dt + jnp.log(-jnp.expm1(-dt))
    gdn_norm_g = 1.0 + 0.02 * nrm(ks[11], (N_A, GDN_HEAD_DIM), f32)
    gdn_w_out = nrm(ks[12], (N_A, GDN_WIDTH, D_MODEL), f32) * (GDN_WIDTH ** -0.5 * BETA_INIT)
    diff_w_kv = nrm(ks[13], (D_MODEL, 2 * DIFF_WIDTH), f32) * D_MODEL ** -0.5
    diff_lambda_k = 0.1 * nrm(ks[14], (2, DIFF_HEAD_DIM), f32)
    diff_w_q = nrm(ks[15], (N_B, D_MODEL, DIFF_WIDTH), f32) * D_MODEL ** -0.5
    diff_lambda_q = 0.1 * nrm(ks[16], (N_B, 2, DIFF_HEAD_DIM), f32)
    diff_norm_g = 1.0 + 0.02 * nrm(ks[17], (N_B, 2 * DIFF_HEAD_DIM), f32)
    diff_w_out = nrm(ks[18], (N_B, DIFF_WIDTH, D_MODEL), f32) * (DIFF_WIDTH ** -0.5 * BETA_INIT)
    return {"x": x, "ln_g": ln_g, "ln_b": ln_b,
            "ffn1_w_in": ffn1_w_in, "ffn1_w_out": ffn1_w_out,
            "ffn2_w_in": ffn2_w_in, "ffn2_w_out": ffn2_w_out,
            "gdn_w_in": gdn_w_in, "gdn_conv_w": gdn_conv_w, "gdn_a_log": gdn_a_log,
            "gdn_dt_bias": gdn_dt_bias, "gdn_norm_g": gdn_norm_g, "gdn_w_out": gdn_w_out,
            "diff_w_kv": diff_w_kv, "diff_lambda_k": diff_lambda_k, "diff_w_q": diff_w_q,
            "diff_lambda_q": diff_lambda_q, "diff_norm_g": diff_norm_g, "diff_w_out": diff_w_out}


def reference(x, ln_g, ln_b, ffn1_w_in, ffn1_w_out, ffn2_w_in, ffn2_w_out,
              gdn_w_in, gdn_conv_w, gdn_a_log, gdn_dt_bias, gdn_norm_g, gdn_w_out,
              diff_w_kv, diff_lambda_k, diff_w_q, diff_lambda_q, diff_norm_g, diff_w_out):
    Bsz, T, _ = x.shape
    k_sh = None
    v_sh = None
    for l in range(DEPTH):
        x = layer_norm(ALPHA * x + 0.5 * swiglu(x, ffn1_w_in[l], ffn1_w_out[l]), ln_g[l, 0], ln_b[l, 0])
        if l < N_A:
            mix = gated_deltanet(x, gdn_w_in[l], gdn_conv_w[l], gdn_a_log[l], gdn_dt_bias[l],
                                 gdn_norm_g[l], gdn_w_out[l])
        else:
            j = l - N_A
            lambda_init = 0.8 - 0.6 * math.exp(-0.3 * l)
            mix = diff_attention_layer(x, k_sh, v_sh, diff_w_q[j], diff_lambda_q[j], diff_lambda_k,
                                       diff_norm_g[j], diff_w_out[j], lambda_init)
        x = layer_norm(ALPHA * x + mix, ln_g[l, 1], ln_b[l, 1])
        x = layer_norm(ALPHA * x + 0.5 * swiglu(x, ffn2_w_in[l], ffn2_w_out[l]), ln_g[l, 2], ln_b[l, 2])
        if l == N_A - 1:
            kv = x @ diff_w_kv
            k_sh = kv[..., :DIFF_WIDTH].reshape(Bsz, T, DIFF_HEADS, 2, DIFF_HEAD_DIM)
            v_sh = kv[..., DIFF_WIDTH:].reshape(Bsz, T, DIFF_HEADS, 2 * DIFF_HEAD_DIM)
    return x
```

```python
from contextlib import ExitStack
import math
import numpy as np
import concourse.bass as bass
import concourse.mybir as mybir
from concourse.bass_utils import run_bass_kernel_spmd

F32 = mybir.dt.float32
BF16 = mybir.dt.bfloat16
AF = mybir.ActivationFunctionType
ALU = mybir.AluOpType

D = 1024
DFF = 2816
DEPTH = 4
N_A = 2
ALPHA = (2.0 * DEPTH) ** 0.25
LN_EPS = 1e-5
NCORES = 8
SEQ = 8192
NEG = -1.0e30


class Op:
    __slots__ = ("eng", "fn", "deps", "sig", "idx", "dsem", "dval")


class Prog:
    CE = ("pe", "act", "dve", "pool")
    ALLE = ("pe", "act", "dve", "pool", "sp")
    BLK = {"pe": "tensor", "act": "scalar", "dve": "vector", "pool": "gpsimd", "sp": "sync"}

    def __init__(self, nc):
        self.nc = nc
        self.esem = {e: nc.alloc_semaphore("sem_" + e) for e in self.CE}
        self.ecnt = {e: 0 for e in self.CE}
        self.dsems = {}
        self.waited = {e: {} for e in self.ALLE}
        self.nops = 0
        self.reset()

    def reset(self):
        self.ops = []
        self.last_w = {}
        self.readers = {}
        self.phase_dma = {}

    def _collect(self, reads, writes):
        deps = []
        for k in reads:
            for o in self.last_w.get(k, ()):
                deps.append(o)
        for k in writes:
            for o in self.last_w.get(k, ()):
                deps.append(o)
            for o in self.readers.get(k, ()):
                deps.append(o)
        return deps

    def _record(self, o, reads, writes):
        isdma = o.dsem is not None
        for k in reads:
            lst = self.readers.setdefault(k, [])
            if not isdma:
                lst[:] = [p for p in lst if not (p.dsem is None and p.eng == o.eng)]
            lst.append(o)
        for k in writes:
            self.readers[k] = []
            self.last_w[k] = [o]

    def op(self, eng, fn, reads=(), writes=(), dsem=None, dval=0, nodeps=False):
        o = Op()
        o.eng = eng
        o.fn = fn
        o.sig = False
        o.idx = 0
        o.dsem = dsem
        o.dval = dval
        deps = [] if nodeps else self._collect(reads, writes)
        o.deps = []
        seen = set()
        for d in deps:
            if id(d) in seen or d is o:
                continue
            seen.add(id(d))
            if d.dsem is None:
                if d.eng == "pe" and eng == "pe":
                    continue
                d.sig = True
            o.deps.append(d)
        self._record(o, reads, writes)
        self.ops.append(o)
        return o

    def dma(self, q, semkey, out, in_, reads=(), writes=(), nodeps=False):
        if semkey not in self.dsems:
            self.dsems[semkey] = [self.nc.alloc_semaphore("d_" + semkey), 0]
        ent = self.dsems[semkey]
        ent[1] += 16
        o = self.op(q, lambda e, out=out, in_=in_: e.dma_start(out=out, in_=in_), reads, writes,
                    dsem=ent[0], dval=ent[1], nodeps=nodeps)
        self.phase_dma[(q, semkey)] = (ent[0], ent[1])
        return o

    def flush(self):
        for o in self.ops:
            if o.dsem is None and o.sig:
                self.ecnt[o.eng] += 1
                o.idx = self.ecnt[o.eng]
        per = {e: [] for e in self.ALLE}
        for o in self.ops:
            per[o.eng].append(o)
        self.nops += len(self.ops)
        with self.nc.Block() as blk:
            for e in self.ALLE:
                def body(eng, e=e, ops=per[e]):
                    waited = self.waited[e]
                    for o in ops:
                        for d in o.deps:
                            if d.dsem is not None:
                                sem, val, key = d.dsem, d.dval, id(d.dsem)
                            else:
                                sem, val, key = self.esem[d.eng], d.idx, d.eng
                            if waited.get(key, 0) < val:
                                eng.wait_ge(sem, val)
                                waited[key] = val
                        ins = o.fn(eng)
                        if o.dsem is not None:
                            ins.then_inc(o.dsem, 16)
                        elif o.sig:
                            ins.then_inc(self.esem[e], 1)
                    for (q, _k), (sem, val) in self.phase_dma.items():
                        if q == e and waited.get(id(sem), 0) < val:
                            eng.wait_ge(sem, val)
                            waited[id(sem)] = val
                getattr(blk, self.BLK[e])(body)
        self.reset()


C_IDENT = 0
C_ONES = 128
C_TRIADD4 = 256
C_STRICT4 = 768
C_IDENT4 = 1280
C_AMASK = 1792
C_SEL = 3840
C_SCAN = 4864
C_I8 = 5376
C_SGN = 5384
C_COLS = 5512


def make_consts():
    c = np.zeros((128, C_COLS), np.float32)
    c[:, C_IDENT:C_IDENT + 128] = np.eye(128, dtype=np.float32)
    c[:, C_ONES:C_ONES + 128] = 1.0
    s = np.arange(128)[:, None]
    cc = np.arange(128)[None, :]
    tri = np.where(s <= cc, 0.0, NEG).astype(np.float32)
    strict = (s < cc).astype(np.float32)
    for r in range(4):
        c[:, C_TRIADD4 + r * 128:C_TRIADD4 + (r + 1) * 128] = tri
        c[:, C_STRICT4 + r * 128:C_STRICT4 + (r + 1) * 128] = strict
        c[:, C_IDENT4 + r * 128:C_IDENT4 + (r + 1) * 128] = np.eye(128, dtype=np.float32)
        q = np.arange(512)[None, :]
        c[:, C_AMASK + r * 512:C_AMASK + (r + 1) * 512] = ((r * 128 + s) <= q).astype(np.float32)
    for h in range(8):
        c[h, C_SEL + h * 128:C_SEL + (h + 1) * 128] = 1.0
    t = np.arange(512)
    c[0:8, C_SCAN:C_SCAN + 512] = (t % 128 != 0).astype(np.float32)[None, :]
    c[0:8, C_I8:C_I8 + 8] = np.eye(8, dtype=np.float32)
    c[0, C_SGN:C_SGN + 128] = -1.0
    c[1, C_SGN:C_SGN + 128] = 1.0
    return c


class Builder:
    def __init__(self, T, nphases=None, debug=False, plan=None):
        self.plan = plan
        self.T = T
        self.NST = T // 512
        self.nphases_limit = nphases
        self.nc = bass.Bass("TRN2", target_bir_lowering=False)
        nc = self.nc
        dt = lambda name, shape, dtype, kind: nc.dram_tensor(name, shape, dtype, kind=kind).ap()
        self.x_in = dt("x", [T, D], F32, "ExternalInput")
        self.consts_d = dt("consts", [128, C_COLS], F32, "ExternalInput")
        self.ln_g = dt("ln_g", [DEPTH, 3, D], F32, "ExternalInput")
        self.ln_b = dt("ln_b", [DEPTH, 3, D], F32, "ExternalInput")
        self.ffn1_w_in = dt("ffn1_w_in", [DEPTH, D, 2 * DFF], F32, "ExternalInput")
        self.ffn1_w_out = dt("ffn1_w_out", [DEPTH, DFF, D], F32, "ExternalInput")
        self.ffn2_w_in = dt("ffn2_w_in", [DEPTH, D, 2 * DFF], F32, "ExternalInput")
        self.ffn2_w_out = dt("ffn2_w_out", [DEPTH, DFF, D], F32, "ExternalInput")
        self.gdn_w_in = dt("gdn_w_in", [N_A, D, 4112], F32, "ExternalInput")
        self.gdn_conv_w = dt("gdn_conv_w", [N_A, 4, 3072], F32, "ExternalInput")
        self.gdn_a_log = dt("gdn_a_log", [N_A, 8], F32, "ExternalInput")
        self.gdn_dt_bias = dt("gdn_dt_bias", [N_A, 8], F32, "ExternalInput")
        self.gdn_norm_g = dt("gdn_norm_g", [N_A, 128], F32, "ExternalInput")
        self.gdn_w_out = dt("gdn_w_out", [N_A, D, D], F32, "ExternalInput")
        self.diff_w_kv = dt("diff_w_kv", [D, 2 * D], F32, "ExternalInput")
        self.diff_lambda_k = dt("diff_lambda_k", [2, 64], F32, "ExternalInput")
        self.diff_w_q = dt("diff_w_q", [2, D, D], F32, "ExternalInput")
        self.diff_lambda_q = dt("diff_lambda_q", [2, 2, 64], F32, "ExternalInput")
        self.diff_norm_g = dt("diff_norm_g", [2, 128], F32, "ExternalInput")
        self.diff_w_out = dt("diff_w_out", [2, D, D], F32, "ExternalInput")
        self.y_out = dt("y", [T, D], F32, "ExternalOutput")
        self.xa = dt("xa", [T, D], F32, "Internal")
        self.xb = dt("xb", [T, D], F32, "Internal")
        skind = "ExternalOutput" if debug else "Internal"
        self.KT = dt("KT", [8, 128, T], BF16, skind)
        self.VD = dt("VD", [8, 128, T], BF16, skind)
        self.QT = dt("QT", [8, 128, T], BF16, skind)
        self.OT = dt("OT", [8, 128, T], BF16, skind)
        NCH = T // 128
        self.G = {}
        for nm in ("gq", "gqg", "gkt", "gkb", "gkbg", "gkd", "gvb"):
            self.G[nm] = dt(nm, [NCH, 128, 1024], BF16, skind)
        self.G["gsz"] = dt("gsz", [NCH, 128, 1024], F32, skind)
        self.GGC = dt("ggc", [8, T], F32, skind)
        self.P = Prog(nc)
        self.nphase = 0
        self.stack = ExitStack()
        sb = lambda name, shape, dtype: nc.alloc_sbuf_tensor(name, shape, dtype)
        self.ident_f = sb("ident_f", [128, 128], F32)
        self.ident_b = sb("ident_b", [128, 128], BF16)
        self.ones_f = sb("ones_f", [128, 128], F32)
        self.ones_b = sb("ones_b", [128, 128], BF16)
        self.epsln = sb("epsln", [128, 1], F32)
        self.ps = [nc.alloc_psum_tensor(f"ps{i}", [128, 512], F32) for i in range(8)]

    def more(self):
        return self.nphases_limit is None or self.nphase < self.nphases_limit

    def end_phase(self):
        self.P.flush()
        self.nphase += 1

    def phase_setup(self):
        P = self.P
        cd = self.consts_d
        P.dma("sp", "c0", self.ident_f[:], cd[:, C_IDENT:C_IDENT + 128], writes=["ident_f"])
        P.dma("sp", "c1", self.ones_f[:], cd[:, C_ONES:C_ONES + 128], writes=["ones_f"])
        P.op("dve", lambda e: e.tensor_copy(out=self.ident_b[:], in_=self.ident_f[:]),
             reads=["ident_f"], writes=["ident_b"])
        P.op("dve", lambda e: e.tensor_copy(out=self.ones_b[:], in_=self.ones_f[:]),
             reads=["ones_f"], writes=["ones_b"])
        P.op("dve", lambda e: e.memset(self.epsln[:], LN_EPS / (ALPHA * ALPHA)), writes=["epsln"])
        self.end_phase()

    def alloc_ln(self, st, A, l, j, nz=1):
        nc = self.nc
        S = lambda name, shape, dtype: st.enter_context(nc.sbuf_tensor(f"{name}_p{self.nphase}", shape, dtype))
        A["z"] = [S(f"z{i}", [128, D], F32) for i in range(nz)]
        A["o"] = [S("o0", [128, D], F32), S("o1", [128, D], F32)]
        A["st6"] = S("st6", [128, 12], F32)
        A["mv"] = S("mv", [128, 2], F32)
        A["sd"] = S("sd", [128, 1], F32)
        A["rstd"] = S("rstd", [128, 1], F32)
        A["nmr"] = S("nmr", [128, 1], F32)
        A["gbc"] = S("gbc", [128, D], F32)
        A["bbc"] = S("bbc", [128, D], F32)
        self.P.dma("sp", "lng", A["gbc"][:], self.ln_g[l, j:j + 1, :].to_broadcast([128, D]), writes=["gbc"])
        self.P.dma("sp", "lnb", A["bbc"][:], self.ln_b[l, j:j + 1, :].to_broadcast([128, D]), writes=["bbc"])

    def load_x(self, x_src, t0, xinb):
        self.P.dma("pool", "xinb", xinb[:], x_src[t0:t0 + 512, :].rearrange("(s p) d -> p s d", p=128),
                   writes=["xinb"])

    def emit_xT(self, x_src, t0, xinb, xT, prefetch=True):
        P, ps = self.P, self.ps
        if t0 == 0:
            self.load_x(x_src, 0, xinb)
        for kc in range(8):
            bank = ps[kc % 4]
            pb = bank.bitcast(BF16)
            for s in range(4):
                P.op("pe", lambda e, pb=pb, s=s, kc=kc: e.transpose(
                    out=pb[:, s * 128:(s + 1) * 128], in_=xinb[:, s, kc * 128:(kc + 1) * 128],
                    identity=self.ident_b[:]),
                    reads=["xinb", "ident_b"], writes=[f"ps{kc % 4}"])
            if kc % 2 == 0:
                P.op("dve", lambda e, pb=pb, kc=kc: e.tensor_copy(out=xT[:, kc, :], in_=pb[:, 0:512]),
                     reads=[f"ps{kc % 4}"], writes=[f"xT{kc}"])
            else:
                P.op("act", lambda e, pb=pb, kc=kc: e.copy(out=xT[:, kc, :], in_=pb[:, 0:512]),
                     reads=[f"ps{kc % 4}"], writes=[f"xT{kc}"])
        if prefetch and t0 + 512 < self.T:
            self.load_x(x_src, t0 + 512, xinb)

    def evac(self, i, out_ap, in_ap, reads, writes, scale=None):
        P = self.P
        if i % 2 == 0:
            if scale is None:
                P.op("dve", lambda e: e.tensor_copy(out=out_ap, in_=in_ap), reads=reads, writes=writes)
            else:
                P.op("dve", lambda e: e.tensor_scalar(out=out_ap, in0=in_ap, scalar1=scale, scalar2=None,
                                                      op0=ALU.mult), reads=reads, writes=writes)
        else:
            if scale is None:
                P.op("act", lambda e: e.copy(out=out_ap, in_=in_ap), reads=reads, writes=writes)
            else:
                P.op("act", lambda e: e.mul(out=out_ap, in_=in_ap, mul=scale), reads=reads, writes=writes)

    def phase_ffn(self, x_src, x_dst, w_in_d, w_out_d, l, j):
        nc, P = self.nc, self.P
        NST = self.NST
        with ExitStack() as st:
            S = lambda name, shape, dtype: st.enter_context(nc.sbuf_tensor(f"{name}_p{self.nphase}", shape, dtype))
            A = {}
            w1 = S("w1", [128, 8, 2 * DFF], BF16)
            w2 = S("w2", [128, 22, D], BF16)
            xinb = S("xinb", [128, 4, D], BF16)
            xres = [S("xres0", [128, D], F32), S("xres1", [128, D], F32)]
            xT = S("xT", [128, 8, 512], BF16)
            act = S("act", [128, 22, 512], BF16)
            sg = [S("sg0", [128, 512], F32), S("sg1", [128, 512], F32)]
            self.alloc_ln(st, A, l, j)
            ps = self.ps
            Y = [None, None]
            w_in_v = w_in_d.rearrange("(c p) f -> p c f", p=128)
            for cg in range(11):
                c0 = cg * 256
                P.dma("pool", f"wl{cg}", w1[:, :, c0:c0 + 256], w_in_v[:, :, c0:c0 + 256], writes=[f"w1g{cg}"])
                P.dma("pool", f"wl{cg}", w1[:, :, DFF + c0:DFF + c0 + 256], w_in_v[:, :, DFF + c0:DFF + c0 + 256],
                      writes=[f"w1g{cg}", f"w1u{cg}"], nodeps=True)
            w_out_v = w_out_d.rearrange("(c p) f -> p c f", p=128)
            for g in range(2):
                P.dma("pool", f"wl{11 + g}", w2[:, g * 11:(g + 1) * 11, :], w_out_v[:, g * 11:(g + 1) * 11, :],
                      writes=[f"w2_{g}"])
            cmul = 0.5 / ALPHA
            self.emit_xT(x_src, 0, xinb, xT)
            for t in range(NST):
                t0 = t * 512
                for jf in range(22):
                    G = ps[(jf % 2) * 2]
                    U = ps[(jf % 2) * 2 + 1]
                    gk, uk = f"ps{(jf % 2) * 2}", f"ps{(jf % 2) * 2 + 1}"
                    for kc in range(8):
                        P.op("pe", lambda e, G=G, kc=kc, jf=jf: e.matmul(
                            G[:], lhsT=w1[:, kc, jf * 128:(jf + 1) * 128], rhs=xT[:, kc, :],
                            start=(kc == 0), stop=(kc == 7)),
                            reads=[f"w1g{jf // 2}", f"xT{kc}"], writes=[gk])
                    for kc in range(8):
                        P.op("pe", lambda e, U=U, kc=kc, jf=jf: e.matmul(
                            U[:], lhsT=w1[:, kc, DFF + jf * 128:DFF + (jf + 1) * 128], rhs=xT[:, kc, :],
                            start=(kc == 0), stop=(kc == 7)),
                            reads=[f"w1u{jf // 2}", f"xT{kc}"], writes=[uk])
                    sgt = sg[jf % 2]
                    P.op("act", lambda e, G=G, sgt=sgt: e.activation(out=sgt[:], in_=G[:], func=AF.Silu),
                         reads=[gk], writes=[f"sg{jf % 2}"])
                    P.op("dve", lambda e, U=U, sgt=sgt, jf=jf: e.tensor_tensor(
                        out=act[:, jf, :], in0=sgt[:], in1=U[:], op=ALU.mult),
                        reads=[uk, f"sg{jf % 2}"], writes=[f"act{jf}"])
                for s in range(4):
                    slot = s % 2
                    r0 = t0 + s * 128
                    P.dma("sp", f"xres{slot}", xres[slot][:], x_src[r0:r0 + 128, :], writes=[f"xres{slot}"])
                    ya, yb = ps[4 + slot * 2], ps[5 + slot * 2]
                    yk = [f"ps{4 + slot * 2}", f"ps{5 + slot * 2}"]
                    for half, yp in enumerate((ya, yb)):
                        for jf in range(22):
                            P.op("pe", lambda e, yp=yp, jf=jf, s=s, half=half: e.matmul(
                                yp[:], lhsT=act[:, jf, s * 128:(s + 1) * 128],
                                rhs=w2[:, jf, half * 512:(half + 1) * 512],
                                start=(jf == 0), stop=(jf == 21)),
                                reads=[f"act{jf}", f"w2_{jf // 11}"], writes=[yk[half]])
                    self.ln_epilogue_2bank(A, ya, yb, yk, xres[slot], f"xres{slot}", cmul, slot,
                                           x_dst[r0:r0 + 128, :], "st")
                    if s == 1 and t + 1 < NST:
                        self.emit_xT(x_src, t0 + 512, xinb, xT)
            self.end_phase()

    def ln_epilogue_2bank(self, A, ya, yb, yk, xres, xres_key, cmul, slot, out_dram, semkey):
        P = self.P
        zi = slot % len(A["z"])
        z = A["z"][zi]
        o = A["o"][slot]
        st6, mv, sd, rstd, nmr = A["st6"], A["mv"], A["sd"], A["rstd"], A["nmr"]
        gbc, bbc = A["gbc"], A["bbc"]
        for half, yp in enumerate((ya, yb)):
            sl = slice(half * 512, (half + 1) * 512)
            P.op("dve", lambda e, yp=yp, sl=sl: e.scalar_tensor_tensor(
                out=z[:, sl], in0=yp[:], scalar=cmul, in1=xres[:, sl], op0=ALU.mult, op1=ALU.add),
                reads=[yk[half], xres_key], writes=[f"z{zi}_{half}"])
            P.op("dve", lambda e, sl=sl, half=half: e.bn_stats(out=st6[:, half * 6:(half + 1) * 6], in_=z[:, sl]),
                 reads=[f"z{zi}_{half}"], writes=[f"st6{half}"])
        P.op("dve", lambda e: e.bn_aggr(out=mv[:], in_=st6[:]), reads=["st60", "st61"], writes=["mv"])
        P.op("act", lambda e: e.activation(out=sd[:], in_=mv[:, 1:2], func=AF.Sqrt, bias=self.epsln[:], scale=1.0),
             reads=["mv", "epsln"], writes=["sd"])
        P.op("dve", lambda e: e.reciprocal(out=rstd[:], in_=sd[:]), reads=["sd"], writes=["rstd"])
        P.op("dve", lambda e: e.scalar_tensor_tensor(out=nmr[:], in0=mv[:, 0:1], scalar=-1.0, in1=rstd[:],
                                                     op0=ALU.mult, op1=ALU.mult),
             reads=["mv", "rstd"], writes=["nmr"])
        ok = f"o{slot}"
        P.op("act", lambda e: e.activation(out=o[:], in_=z[:], func=AF.Identity, bias=nmr[:], scale=rstd[:]),
             reads=[f"z{zi}_0", f"z{zi}_1", "rstd", "nmr"], writes=[ok])
        P.op("pool", lambda e: e.tensor_tensor(out=o[:], in0=o[:], in1=gbc[:], op=ALU.mult),
             reads=[ok, "gbc"], writes=[ok])
        P.op("pool", lambda e: e.tensor_tensor(out=o[:], in0=o[:], in1=bbc[:], op=ALU.add),
             reads=[ok, "bbc"], writes=[ok])
        P.dma("sp", f"{semkey}{slot}", out_dram, o[:], reads=[ok])

    def phase_proj_fm(self, x_src, w_d, col0, ncols_chunks, dst, scale, also_v=None):
        nc, P, ps = self.nc, self.P, self.ps
        NST = self.NST
        wcols = w_d.shape[1]
        with ExitStack() as st:
            S = lambda name, shape, dtype: st.enter_context(nc.sbuf_tensor(f"{name}_p{self.nphase}", shape, dtype))
            w = S("wp", [128, 8, wcols], BF16)
            xinb = S("xinb", [128, 4, D], BF16)
            xT = S("xT", [128, 8, 512], BF16)
            kst = [S("kst0", [128, 8, 512], BF16), S("kst1", [128, 8, 512], BF16)]
            if also_v is not None:
                vst = [S("vst0", [128, 8, 4, 128], BF16), S("vst1", [128, 8, 4, 128], BF16)]
            w_v = w_d.rearrange("(c p) f -> p c f", p=128)
            for kc in range(8):
                P.dma("pool", f"wl{kc}", w[:, kc, :], w_v[:, kc, :], writes=[f"wp{kc}"])
            ev = 0
            for t in range(NST):
                t0 = t * 512
                slot = t % 2
                self.emit_xT(x_src, t0, xinb, xT)
                for hc in range(ncols_chunks):
                    b = 4 + hc % 4
                    bank = ps[b]
                    for kc in range(8):
                        P.op("pe", lambda e, bank=bank, kc=kc, hc=hc: e.matmul(
                            bank[:], lhsT=w[:, kc, col0 + hc * 128:col0 + (hc + 1) * 128], rhs=xT[:, kc, :],
                            start=(kc == 0), stop=(kc == 7)),
                            reads=[f"wp{kc}", f"xT{kc}"], writes=[f"ps{b}"])
                    self.evac(ev, kst[slot][:, hc, :], bank[:], [f"ps{b}"], [f"kst{slot}"], scale=scale)
                    ev += 1
                P.dma("sp", f"kst{slot}", dst[:, :, t0:t0 + 512].rearrange("h p t -> p h t"), kst[slot][:],
                      reads=[f"kst{slot}"])
                if also_v is not None:
                    vc0 = also_v
                    for s_ in range(4):
                        for half in range(2):
                            b = 4 + (s_ * 2 + half) % 4
                            bank = ps[b]
                            for kc in range(8):
                                P.op("pe", lambda e, bank=bank, kc=kc, s_=s_, half=half: e.matmul(
                                    bank[:], lhsT=xT[:, kc, s_ * 128:(s_ + 1) * 128],
                                    rhs=w[:, kc, vc0 + half * 512:vc0 + (half + 1) * 512],
                                    start=(kc == 0), stop=(kc == 7)),
                                    reads=[f"wp{kc}", f"xT{kc}"], writes=[f"ps{b}"])
                            self.evac(ev, vst[slot][:, half * 4:(half + 1) * 4, s_, :],
                                      bank[:].rearrange("p (h v) -> p h v", h=4),
                                      [f"ps{b}"], [f"vst{slot}"])
                            ev += 1
                    P.dma("sp", f"vst{slot}", self.VD[:, :, t0:t0 + 512].rearrange("h p f -> p h f"),
                          vst[slot][:].rearrange("p h s v -> p h (s v)"), reads=[f"vst{slot}"])
            self.end_phase()

    def phase_attn(self, jl, lambda_init):
        nc, P, ps = self.nc, self.P, self.ps
        T = self.T
        NQ = T // 512
        X = mybir.AxisListType.X
        with ExitStack() as st:
            S = lambda name, shape, dtype: st.enter_context(nc.sbuf_tensor(f"{name}_p{self.nphase}", shape, dtype))
            kT = S("kT", [128, T], BF16)
            vv = S("vv", [128, T], BF16)
            qT = S("qT", [128, T], BF16)
            oT = S("oT", [128, T], BF16)
            PT = [[S(f"pt{c}{k}", [128, 512], BF16) for k in range(3)] for c in range(2)]
            lacc = [S("lacc0", [128, 512], F32), S("lacc1", [128, 512], F32)]
            amask = S("amask", [128, 4, 512], BF16)
            tmp = {k: S("e_" + k, [128, 512], F32) for k in ("r0", "r1", "a", "b", "c", "d", "e", "f", "g", "h")}
            lq = S("lq", [2, 64], F32)
            lk = S("lk", [2, 64], F32)
            lprod = S("lprod", [2, 64], F32)
            lsum = S("lsum", [2, 1], F32)
            lexp = S("lexp", [2, 1], F32)
            sgn = S("sgn", [2, 128], F32)
            nlam = S("nlam", [128, 1], F32)
            graw = S("graw", [128, 1], F32)
            gs = S("gs", [128, 1], F32)
            epsn = S("epsn", [128, 1], F32)
            P.dma("pool", "amask", amask[:], self.consts_d[:, C_AMASK:C_AMASK + 2048].rearrange("p (r q) -> p r q", r=4),
                  writes=["amask"])
            P.dma("sp", "lq", lq[:], self.diff_lambda_q[jl], writes=["lq"])
            P.dma("sp", "lk", lk[:], self.diff_lambda_k, writes=["lk"])
            P.dma("sp", "sgn", sgn[:], self.consts_d[0:2, C_SGN:C_SGN + 128], writes=["sgn"])
            P.dma("sp", "graw", graw[:], self.diff_norm_g[jl].rearrange("(p o) -> p o", o=1), writes=["graw"])
            P.op("dve", lambda e: e.memset(epsn[:], 1e-5), writes=["epsn"])
            P.op("dve", lambda e: e.tensor_tensor(out=lprod[:], in0=lq[:], in1=lk[:], op=ALU.mult),
                 reads=["lq", "lk"], writes=["lprod"])
            P.op("dve", lambda e: e.tensor_reduce(out=lsum[:], in_=lprod[:], axis=X, op=ALU.add),
                 reads=["lprod"], writes=["lsum"])
            P.op("act", lambda e: e.activation(out=lexp[:], in_=lsum[:], func=AF.Exp), reads=["lsum"], writes=["lexp"])
            P.op("pe", lambda e: e.matmul(ps[0][:, 0:1], lhsT=sgn[:], rhs=lexp[:], start=True, stop=True),
                 reads=["sgn", "lexp"], writes=["ps0"])
            P.op("dve", lambda e: e.tensor_scalar(out=nlam[:], in0=ps[0][:, 0:1], scalar1=-float(lambda_init),
                                                  scalar2=None, op0=ALU.add), reads=["ps0"], writes=["nlam"])
            P.op("dve", lambda e: e.tensor_scalar(out=gs[:], in0=graw[:], scalar1=float(1.0 - lambda_init),
                                                  scalar2=None, op0=ALU.mult), reads=["graw"], writes=["gs"])
            pending = []
            for h in range(8):
                P.dma("sp", "kT", kT[:], self.KT[h], writes=["kT"])
                P.dma("sp", "vv", vv[:], self.VD[h], writes=["vv"])
                P.dma("sp", "qT", qT[:], self.QT[h], writes=["qT"])
                for i in range(NQ):
                    nj = 4 * i + 4
                    qsl = slice(i * 512, (i + 1) * 512)

                    def qk(j, i=i, qsl=qsl):
                        for c in range(2):
                            b = (j % 2) if c == 0 else (2 + j % 3)
                            bank = ps[b]
                            pt = PT[c][j % 3]
                            pk = f"pt{c}{j % 3}"
                            self.o_mm(bank[:], kT[c * 64:(c + 1) * 64, j * 128:(j + 1) * 128],
                                      qT[c * 64:(c + 1) * 64, qsl], True, True, ["kT", "qT"], [f"ps{b}"])
                            self.o_act(pt[:], bank[:], AF.Exp, [f"ps{b}"], [pk])
                            if j >= 4 * i:
                                r = j - 4 * i
                                self.o_tt("dve", pt[:], pt[:], amask[:, r, :], ALU.mult, [pk, "amask"], [pk])
                            if c == 1:
                                if j == 0:
                                    self.o_cp("dve", lacc[c][:], pt[:], [pk], [f"lacc{c}"])
                                else:
                                    self.o_tt("dve", lacc[c][:], lacc[c][:], pt[:], ALU.add, [pk, f"lacc{c}"], [f"lacc{c}"])

                    def pv(j, nj=nj):
                        for c in range(2):
                            pt = PT[c][j % 3]
                            pk = f"pt{c}{j % 3}"
                            self.o_mm(ps[6 + c][:], vv[:, j * 128:(j + 1) * 128], pt[:], j == 0, j == nj - 1,
                                      ["vv", pk], [f"ps{6 + c}"])
                            if c == 0:
                                self.o_mm(ps[5][:], self.ones_b[:], pt[:], j == 0, j == nj - 1, ["ones_b", pk], ["ps5"])

                    qk(0)
                    if nj > 1:
                        qk(1)
                    for j in range(nj):
                        if j + 2 < nj:
                            qk(j + 2)
                        pv(j)
                        if j == min(8, nj - 1):
                            while pending:
                                pending.pop(0)()
                    t_ = tmp
                    self.o_cp("dve", t_["a"][:], ps[6][:], ["ps6"], ["e_a"])
                    self.o_cp("dve", t_["b"][:], ps[7][:], ["ps7"], ["e_b"])

                    self.o_mm(ps[3][:], self.ones_f[:], lacc[1][:], True, True, ["ones_f", "lacc1"], ["ps3"])
                    self.o_cp("dve", t_["g"][:], ps[5][:], ["ps5"], ["e_g"])
                    self.o_cp("dve", t_["h"][:], ps[3][:], ["ps3"], ["e_h"])
                    self.o_rcp(t_["r0"][:], t_["g"][:], ["e_g"], ["e_r0"])
                    self.o_rcp(t_["r1"][:], t_["h"][:], ["e_h"], ["e_r1"])
                    self.o_tt("dve", t_["a"][:], t_["a"][:], t_["r0"][:], ALU.mult, ["e_a", "e_r0"], ["e_a"])
                    self.o_tt("dve", t_["b"][:], t_["b"][:], t_["r1"][:], ALU.mult, ["e_b", "e_r1"], ["e_b"])
                    self.o_stt(t_["c"][:], t_["b"][:], nlam[:], t_["a"][:], ALU.mult, ALU.add, ["e_a", "e_b", "nlam"], ["e_c"])
                    self.o_tt("dve", t_["d"][:], t_["c"][:], t_["c"][:], ALU.mult, ["e_c"], ["e_d"])

                    def tail(qsl=qsl):
                        self.o_mm(ps[1][:], self.ones_f[:], t_["d"][:], True, True, ["ones_f", "e_d"], ["ps1"])
                        self.o_act(t_["e"][:], ps[1][:], AF.Ln, ["ps1", "epsn"], ["e_e"], bias=epsn[:], scale=1.0 / 128.0)
                        self.o_act(t_["f"][:], t_["e"][:], AF.Exp, ["e_e"], ["e_f"], scale=-0.5)
                        self.o_stt(oT[:, qsl], t_["c"][:], gs[:], t_["f"][:], ALU.mult, ALU.mult, ["e_c", "e_f", "gs"], ["oT"])
                    pending.append(tail)
                while pending:
                    pending.pop(0)()
                P.dma("sp", "oT", self.OT[h], oT[:], reads=["oT"])
            self.end_phase()

    def phase_outproj(self, x_src, x_dst, w_d, src_fm, l):
        nc, P, ps = self.nc, self.P, self.ps
        NST = self.NST
        with ExitStack() as st:
            S = lambda name, shape, dtype: st.enter_context(nc.sbuf_tensor(f"{name}_p{self.nphase}", shape, dtype))
            A = {}
            wo = S("wo", [128, 8, D], BF16)
            ot = [S("ot0", [128, 8, 512], BF16), S("ot1", [128, 8, 512], BF16)]
            xres = [S("xres0", [128, D], F32), S("xres1", [128, D], F32)]
            self.alloc_ln(st, A, l, 1, nz=2)
            w_v = w_d.rearrange("(c p) f -> p c f", p=128)
            for g in range(2):
                P.dma("pool", f"wl{g}", wo[:, g * 4:(g + 1) * 4, :], w_v[:, g * 4:(g + 1) * 4, :], writes=[f"wo{g}"])
            cmul = 1.0 / ALPHA
            for t in range(NST):
                t0 = t * 512
                ts_ = t % 2
                P.dma("sp", f"ot{ts_}", ot[ts_][:], src_fm[:, :, t0:t0 + 512].rearrange("h p t -> p h t"),
                      writes=[f"ot{ts_}"])
                for s_ in range(4):
                    slot = s_ % 2
                    r0 = t0 + s_ * 128
                    if t == 0 and s_ == 0:
                        P.dma("sp", "xres0", xres[0][:], x_src[0:128, :], writes=["xres0"])
                    if r0 + 128 < self.T:
                        ns = 1 - slot
                        P.dma("sp", f"xres{ns}", xres[ns][:], x_src[r0 + 128:r0 + 256, :], writes=[f"xres{ns}"])
                    ya, yb = ps[4 + slot * 2], ps[5 + slot * 2]
                    yk = [f"ps{4 + slot * 2}", f"ps{5 + slot * 2}"]
                    for half, yp in enumerate((ya, yb)):
                        for h in range(8):
                            P.op("pe", lambda e, yp=yp, h=h, s_=s_, half=half, ts_=ts_: e.matmul(
                                yp[:], lhsT=ot[ts_][:, h, s_ * 128:(s_ + 1) * 128],
                                rhs=wo[:, h, half * 512:(half + 1) * 512], start=(h == 0), stop=(h == 7)),
                                reads=[f"ot{ts_}", f"wo{h // 4}"], writes=[yk[half]])
                    self.ln_epilogue_2bank(A, ya, yb, yk, xres[slot], f"xres{slot}", cmul, slot,
                                           x_dst[r0:r0 + 128, :], "st")
            self.end_phase()

    def o_mm(self, out, lhsT, rhs, start, stop, reads, writes):
        self.P.op("pe", lambda e: e.matmul(out, lhsT=lhsT, rhs=rhs, start=start, stop=stop), reads, writes)

    def o_tr(self, out, in_, ident, reads, writes):
        self.P.op("pe", lambda e: e.transpose(out=out, in_=in_, identity=ident), reads, writes)

    def o_tt(self, eng, out, in0, in1, op, reads, writes):
        self.P.op(eng, lambda e: e.tensor_tensor(out=out, in0=in0, in1=in1, op=op), reads, writes)

    def o_stt(self, out, in0, scalar, in1, op0, op1, reads, writes):
        self.P.op("dve", lambda e: e.scalar_tensor_tensor(out=out, in0=in0, scalar=scalar, in1=in1, op0=op0, op1=op1),
                  reads, writes)

    def o_ts(self, eng, out, in0, s1, op0, reads, writes):
        self.P.op(eng, lambda e: e.tensor_scalar(out=out, in0=in0, scalar1=s1, scalar2=None, op0=op0), reads, writes)

    def o_act(self, out, in_, func, reads, writes, bias=None, scale=None):
        kw = {}
        if bias is not None:
            kw["bias"] = bias
        if scale is not None:
            kw["scale"] = scale
        self.P.op("act", lambda e: e.activation(out=out, in_=in_, func=func, **kw), reads, writes)

    def o_cp(self, eng, out, in_, reads, writes):
        if eng == "act":
            self.P.op("act", lambda e: e.copy(out=out, in_=in_), reads, writes)
        else:
            self.P.op(eng, lambda e: e.tensor_copy(out=out, in_=in_), reads, writes)

    def o_rcp(self, out, in_, reads, writes):
        self.P.op("dve", lambda e: e.reciprocal(out=out, in_=in_), reads, writes)

    def o_rcpf(self, out, in_, reads, writes):
        self.P.op("dve", lambda e: e.reciprocal_approx_fast(out=out, in_=in_), reads, writes)

    def phase_gdn_proj(self, x_src, l):
        nc, P, ps = self.nc, self.P, self.ps
        NST = self.NST
        G = self.G
        with ExitStack() as st:
            S = lambda name, shape, dtype: st.enter_context(nc.sbuf_tensor(f"{name}_p{self.nphase}", shape, dtype))
            w = S("gw", [128, 8, 4112], BF16)
            xinb = S("xinb", [128, 4, D], BF16)
            xT = S("xT", [128, 8, 512], BF16)
            cwr = S("cwr", [24, 4, 128], F32)
            cw = S("cw", [128, 4, 24], F32)
            alog = S("alog", [8, 1], F32)
            dtb = S("dtb", [8, 1], F32)
            nexpA = S("nexpA", [8, 1], F32)
            one8 = S("one8", [8, 1], F32)
            eps6 = S("eps6", [128, 1], F32)
            sel = S("sel", [8, 8, 128], F32)
            scanm = S("scanm", [8, 512], F32)
            rows = {k: S("row_" + k, [8, 512], F32) for k in ("bl", "al", "gc", "egc", "ekd", "tmp")}
            pre = [S("pre0", [128, 515], F32), S("pre1", [128, 515], F32), S("pre2", [128, 515], F32)]
            halo = S("halo", [128, 24, 3], F32)
            cacc = [S("cacc0", [128, 512], F32), S("cacc1", [128, 512], F32)]
            qs = [S("qs0", [128, 512], F32), S("qs1", [128, 512], F32)]
            ks = [S("ks0", [128, 512], F32), S("ks1", [128, 512], F32)]
            vs = [S("vs0", [128, 512], F32), S("vs1", [128, 512], F32)]
            sqq = [S("sqq0", [128, 512], F32), S("sqq1", [128, 512], F32)]
            sqk = [S("sqk0", [128, 512], F32), S("sqk1", [128, 512], F32)]
            lnq = S("lnq", [128, 512], F32)
            lnk = S("lnk", [128, 512], F32)
            rsq = S("rsq", [128, 512], F32)
            rsk = S("rsk", [128, 512], F32)
            t1 = S("t1", [128, 512], F32)
            t2 = S("t2", [128, 512], F32)
            t3 = S("t3", [128, 512], F32)
            kbgT = [S("kbgT0", [128, 512], BF16), S("kbgT1", [128, 512], BF16)]
            kdT = [S("kdT0", [128, 512], BF16), S("kdT1", [128, 512], BF16)]
            vbT = [S("vbT0", [128, 512], BF16), S("vbT1", [128, 512], BF16)]
            stg = {nm: S("st_" + nm, [128, 4, 4, 128], BF16) for nm in ("gq", "gqg", "gkt", "gkb", "gkbg", "gkd", "gvb")}
            stg_sz = [S("st_gsz0", [128, 4, 4, 128], F32), S("st_gsz1", [128, 4, 4, 128], F32)]
            w_v = self.gdn_w_in[l].rearrange("(c p) f -> p c f", p=128)
            for kc in range(8):
                P.dma("pool", f"wl{kc}", w[:, kc, :], w_v[:, kc, :], writes=[f"gw{kc}"])
            gwk = [f"gw{kc}" for kc in range(8)]
            P.dma("sp", "cwr", cwr[:], self.gdn_conv_w[l].rearrange("j (c p) -> c j p", p=128), writes=["cwr"])
            P.dma("sp", "alog", alog[:], self.gdn_a_log[l].rearrange("(p o) -> p o", o=1), writes=["alog"])
            P.dma("sp", "dtb", dtb[:], self.gdn_dt_bias[l].rearrange("(p o) -> p o", o=1), writes=["dtb"])
            P.dma("sp", "sel", sel[:], self.consts_d[0:8, C_SEL:C_SEL + 1024].rearrange("k (h j) -> k h j", h=8),
                  writes=["sel"])
            P.dma("sp", "scanm", scanm[:], self.consts_d[0:8, C_SCAN:C_SCAN + 512], writes=["scanm"])
            P.op("dve", lambda e: e.memset(one8[:], 1.0), writes=["one8"])
            P.op("dve", lambda e: e.memset(eps6[:], 1e-6), writes=["eps6"])
            P.op("pool", lambda e: e.memset(halo[:], 0.0), writes=[f"halo{c}" for c in range(24)])
            for j in range(4):
                self.o_tr(ps[4][:, j * 24:(j + 1) * 24], cwr[:, j, :], self.ident_f[0:24, 0:24], ["cwr", "ident_f"], ["ps4"])
            self.o_cp("dve", cw[:], ps[4][:, 0:96].rearrange("p (j c) -> p j c", j=4), ["ps4"], ["cw"])
            self.o_act(nexpA[:], alog[:], AF.Exp, ["alog"], ["nexpA"])
            self.o_ts("dve", nexpA[:], nexpA[:], -1.0, ALU.mult, ["nexpA"], ["nexpA"])
            QSC = 128.0 ** -0.5
            nproj = 0
            for t in range(NST):
                t0 = t * 512
                self.emit_xT(x_src, t0, xinb, xT)
                xTk = [f"xT{kc}" for kc in range(8)]
                for kc in range(8):
                    self.o_mm(ps[4][0:8, :], w[:, kc, 4096:4104], xT[:, kc, :], kc == 0, kc == 7, [gwk[kc], xTk[kc]], ["ps4"])
                for kc in range(8):
                    self.o_mm(ps[5][0:8, :], w[:, kc, 4104:4112], xT[:, kc, :], kc == 0, kc == 7, [gwk[kc], xTk[kc]], ["ps5"])
                R_ = rows
                self.o_act(R_["bl"][:], ps[4][0:8, :], AF.Sigmoid, ["ps4"], ["r_bl"])
                self.o_act(R_["tmp"][:], ps[5][0:8, :], AF.Exp, ["ps5", "dtb"], ["r_tmp"], bias=dtb[:])
                self.o_act(R_["al"][:], R_["tmp"][:], AF.Ln, ["r_tmp", "one8"], ["r_al"], bias=one8[:])
                self.o_ts("dve", R_["al"][:], R_["al"][:], nexpA[:], ALU.mult, ["r_al", "nexpA"], ["r_al"])
                P.op("dve", lambda e: e.tensor_tensor_scan(out=R_["gc"][:], data0=scanm[:], data1=R_["al"][:], initial=0.0,
                                                           op0=ALU.mult, op1=ALU.add),
                     reads=["r_al", "scanm"], writes=["r_gc"])
                self.o_act(R_["egc"][:], R_["gc"][:], AF.Exp, ["r_gc"], ["r_egc"])
                for ch in range(4):
                    self.o_ts("dve", R_["tmp"][:, ch * 128:(ch + 1) * 128], R_["gc"][:, ch * 128:(ch + 1) * 128],
                              R_["gc"][:, ch * 128 + 127:ch * 128 + 128], ALU.subtract, ["r_gc"], ["r_tmp"])
                self.o_act(R_["ekd"][:], R_["tmp"][:], AF.Exp, ["r_tmp"], ["r_ekd"], scale=-1.0)
                P.dma("sp", "ggc", self.GGC[:, t0:t0 + 512], R_["gc"][:], reads=["r_gc"])
                v4 = lambda ap: ap.rearrange("p (n c) -> p n c", n=4)

                def stageA(h):
                    nonlocal nproj
                    hb = h % 2
                    hh = h % 4
                    for which in range(3):
                        cidx = which * 8 + h
                        b = 4 + nproj % 2
                        nproj += 1
                        bank = ps[b]
                        for kc in range(8):
                            self.o_mm(bank[:], w[:, kc, cidx * 128:(cidx + 1) * 128], xT[:, kc, :], kc == 0, kc == 7,
                                      [gwk[kc], xTk[kc]], [f"ps{b}"])
                        pr = pre[which]
                        pk = f"pre{which}"
                        self.o_cp("act", pr[:, 3:515], bank[:], [f"ps{b}"], [pk + "m"])
                        self.o_cp("pool", pr[:, 0:3], halo[:, cidx, :], [f"halo{cidx}"], [pk + "h"])
                        self.o_cp("pool", halo[:, cidx, :], pr[:, 512:515], [pk + "m"], [f"halo{cidx}"])
                    cidx = 24 + h
                    b = 4 + nproj % 2
                    nproj += 1
                    bank = ps[b]
                    for kc in range(8):
                        self.o_mm(bank[:], w[:, kc, cidx * 128:(cidx + 1) * 128], xT[:, kc, :], kc == 0, kc == 7,
                                  [gwk[kc], xTk[kc]], [f"ps{b}"])
                    self.o_act(stg_sz[h // 4][:, :, hh, :], bank[:].rearrange("p (n c) -> p n c", n=4), AF.Silu,
                               [f"ps{b}"], [f"st_gsz{h // 4}"])
                    for which, dstt, dk in ((0, qs[hb], f"qs{hb}"), (1, ks[hb], f"ks{hb}"), (2, vs[hb], f"vs{hb}")):
                        cidx = which * 8 + h
                        pr = pre[which]
                        pk = f"pre{which}"
                        ca = cacc[which % 2]
                        ck = f"cacc{which % 2}"
                        self.o_ts("dve", ca[:], pr[:, 0:512], cw[:, 0, cidx:cidx + 1], ALU.mult,
                                  [pk + "m", pk + "h", "cw"], [ck])
                        for j in range(1, 4):
                            self.o_stt(ca[:], pr[:, j:j + 512], cw[:, j, cidx:cidx + 1], ca[:], ALU.mult, ALU.add,
                                       [pk + "m", pk + "h", "cw", ck], [ck])
                        self.o_act(dstt[:], ca[:], AF.Silu, [ck], [dk])
                        if which == 0:
                            self.o_tt("pool", sqq[hb][:], qs[hb][:], qs[hb][:], ALU.mult, [f"qs{hb}"], [f"sqq{hb}"])
                        elif which == 1:
                            self.o_tt("pool", sqk[hb][:], ks[hb][:], ks[hb][:], ALU.mult, [f"ks{hb}"], [f"sqk{hb}"])

                def stageB1(h):
                    hb = h % 2
                    hh = h % 4
                    self.o_mm(ps[6][:], self.ones_f[:], sqq[hb][:], True, True, ["ones_f", f"sqq{hb}"], ["ps6"])
                    self.o_mm(ps[7][:], self.ones_f[:], sqk[hb][:], True, True, ["ones_f", f"sqk{hb}"], ["ps7"])
                    self.o_mm(ps[0][:], sel[:, h, :], R_["bl"][:], True, True, ["sel", "r_bl"], ["ps0"])
                    self.o_mm(ps[1][:], sel[:, h, :], R_["egc"][:], True, True, ["sel", "r_egc"], ["ps1"])
                    self.o_mm(ps[2][:], sel[:, h, :], R_["ekd"][:], True, True, ["sel", "r_ekd"], ["ps2"])
                    self.o_act(lnq[:], ps[6][:], AF.Ln, ["ps6", "eps6"], ["lnq"], bias=eps6[:])
                    self.o_act(rsq[:], lnq[:], AF.Exp, ["lnq"], ["rsq"], scale=-0.5)
                    self.o_act(lnk[:], ps[7][:], AF.Ln, ["ps7", "eps6"], ["lnk"], bias=eps6[:])
                    self.o_act(rsk[:], lnk[:], AF.Exp, ["lnk"], ["rsk"], scale=-0.5)
                    self.o_stt(t1[:], qs[hb][:], QSC, rsq[:], ALU.mult, ALU.mult, [f"qs{hb}", "rsq"], ["t1"])
                    self.o_cp("act", stg["gq"][:, :, hh, :], v4(t1[:]), ["t1"], ["st_gq"])
                    self.o_tt("dve", stg["gqg"][:, :, hh, :], v4(t1[:]), v4(ps[1][:]), ALU.mult, ["t1", "ps1"], ["st_gqg"])
                    self.o_tt("dve", t2[:], ks[hb][:], rsk[:], ALU.mult, [f"ks{hb}", "rsk"], ["t2"])
                    self.o_cp("act", stg["gkt"][:, :, hh, :], v4(t2[:]), ["t2"], ["st_gkt"])
                    self.o_tt("dve", t3[:], t2[:], ps[0][:], ALU.mult, ["t2", "ps0"], ["t3"])
                    self.o_cp("act", stg["gkb"][:, :, hh, :], v4(t3[:]), ["t3"], ["st_gkb"])
                    self.o_tt("dve", kbgT[hb][:], t3[:], ps[1][:], ALU.mult, ["t3", "ps1"], [f"kbgT{hb}"])
                    self.o_tt("dve", kdT[hb][:], t2[:], ps[2][:], ALU.mult, ["t2", "ps2"], [f"kdT{hb}"])
                    self.o_tt("dve", vbT[hb][:], vs[hb][:], ps[0][:], ALU.mult, [f"vs{hb}", "ps0"], [f"vbT{hb}"])
                    if hh == 3:
                        hg = h // 4
                        for nm in ("gq", "gqg", "gkt", "gkb"):
                            P.dma("sp", "st_" + nm,
                                  G[nm][4 * t:4 * t + 4, :, hg * 512:(hg + 1) * 512].rearrange("n p f -> p n f"),
                                  stg[nm][:].rearrange("p n h c -> p n (h c)"), reads=["st_" + nm])
                        P.dma("sp", f"st_gsz{hg}",
                              G["gsz"][4 * t:4 * t + 4, :, hg * 512:(hg + 1) * 512].rearrange("n p f -> p n f"),
                              stg_sz[hg][:].rearrange("p n h c -> p n (h c)"), reads=[f"st_gsz{hg}"])

                def stageB2(h):
                    hb = h % 2
                    hh = h % 4
                    pb = ps[3].bitcast(BF16)
                    for (src, sk, nm, ev) in ((kbgT[hb], f"kbgT{hb}", "gkbg", "act"), (kdT[hb], f"kdT{hb}", "gkd", "dve"),
                                              (vbT[hb], f"vbT{hb}", "gvb", "act")):
                        for ch in range(4):
                            self.o_tr(pb[:, ch * 128:(ch + 1) * 128], src[:, ch * 128:(ch + 1) * 128], self.ident_b[:],
                                      [sk, "ident_b"], ["ps3"])
                        self.o_cp(ev, stg[nm][:, :, hh, :], pb[:, 0:512].rearrange("p (n c) -> p n c", n=4),
                                  ["ps3"], ["st_" + nm])
                    if hh == 3:
                        hg = h // 4
                        for nm in ("gkbg", "gkd", "gvb"):
                            P.dma("sp", "st_" + nm,
                                  G[nm][4 * t:4 * t + 4, :, hg * 512:(hg + 1) * 512].rearrange("n p f -> p n f"),
                                  stg[nm][:].rearrange("p n h c -> p n (h c)"), reads=["st_" + nm])

                stageA(0)
                for h in range(8):
                    if h + 1 < 8:
                        stageA(h + 1)
                    stageB1(h)
                    if h >= 1:
                        stageB2(h - 1)
                stageB2(7)
            self.end_phase()

    def phase_gdn_core(self, l):
        nc, P, ps = self.nc, self.P, self.ps
        T = self.T
        NCH = T // 128
        G = self.G
        arrs = ("gq", "gqg", "gkt", "gkb", "gkbg", "gkd", "gvb")
        with ExitStack() as st:
            S = lambda name, shape, dtype: st.enter_context(nc.sbuf_tensor(f"{name}_p{self.nphase}", shape, dtype))
            ld = [{nm: S(f"ld{sl}_{nm}", [128, 8, 128], BF16) for nm in arrs} for sl in range(2)]
            ldsz = [S(f"ld{sl}_gsz", [128, 8, 128], F32) for sl in range(2)]
            gcr = [S(f"gcr{sl}", [8, 128], F32) for sl in range(2)]
            triadd = S("triadd", [128, 4, 128], F32)
            strict = S("strict", [128, 4, 128], F32)
            ident4 = S("ident4", [128, 4, 128], F32)
            sel = S("sel", [8, 8, 128], F32)
            seln = S("seln", [8, 8, 128], F32)
            i8 = S("i8", [8, 8], F32)
            glI = S("glI", [8, 8], F32)
            cd = S("cd", [128, 8], F32)
            gn = S("gn", [128, 1], F32)
            epsn = S("epsn", [128, 1], F32)
            Sf = S("Sf", [128, 8, 128], F32)
            Sb = S("Sb", [128, 8, 128], BF16)
            ogst = [S("ogst0", [128, 8, 512], BF16), S("ogst1", [128, 8, 512], BF16)]
            W = []
            for g in range(2):
                d = {}
                for nm in ("tmpD", "E", "Es", "Na", "Nb", "Ma", "Mb", "Ua", "Ub2", "sq", "rstd", "tt"):
                    d[nm] = S(f"g{g}_{nm}", [128, 4, 128], F32)
                for nm in ("attnT", "Ubf", "nwT", "vn"):
                    d[nm] = S(f"g{g}_{nm}", [128, 4, 128], BF16)
                W.append(d)
            cdr = self.consts_d
            P.dma("sp", "c_tri", triadd[:], cdr[:, C_TRIADD4:C_TRIADD4 + 512].rearrange("p (h c) -> p h c", h=4), writes=["triadd"])
            P.dma("sp", "c_str", strict[:], cdr[:, C_STRICT4:C_STRICT4 + 512].rearrange("p (h c) -> p h c", h=4), writes=["strict"])
            P.dma("sp", "c_id4", ident4[:], cdr[:, C_IDENT4:C_IDENT4 + 512].rearrange("p (h c) -> p h c", h=4), writes=["ident4"])
            P.dma("sp", "c_sel", sel[:], cdr[0:8, C_SEL:C_SEL + 1024].rearrange("k (h j) -> k h j", h=8), writes=["sel"])
            P.dma("sp", "c_i8", i8[:], cdr[0:8, C_I8:C_I8 + 8], writes=["i8"])
            P.dma("sp", "c_gn", gn[:], self.gdn_norm_g[l].rearrange("(p o) -> p o", o=1), writes=["gn"])
            self.o_ts("dve", seln[:], sel[:], -1.0, ALU.mult, ["sel"], ["seln"])
            P.op("dve", lambda e: e.memset(epsn[:], 1e-5), writes=["epsn"])
            P.op("dve", lambda e: e.memset(Sf[:], 0.0), writes=["Sf0", "Sf1"])
            P.op("pool", lambda e: e.memset(Sb[:], 0.0), writes=["Sb0", "Sb1"])
            v4 = lambda bank: bank[:].rearrange("p (h c) -> p h c", h=4)

            def load(n):
                sl = n % 2
                for nm in arrs:
                    P.dma("sp", f"ld{sl}_{nm}", ld[sl][nm][:].rearrange("p h c -> p (h c)"), G[nm][n], writes=[f"ld{sl}_{nm}"])
                P.dma("sp", f"ld{sl}_gsz", ldsz[sl][:].rearrange("p h c -> p (h c)"), G["gsz"][n], writes=[f"ld{sl}_gsz"])
                P.dma("sp", f"gcr{sl}", gcr[sl][:], self.GGC[:, n * 128:(n + 1) * 128], writes=[f"gcr{sl}"])

            load(0)
            for n in range(NCH):
                sl = n % 2
                if n + 1 < NCH:
                    load(n + 1)
                L = ld[sl]
                lk = lambda nm: f"ld{sl}_{nm}"
                gk = f"gcr{sl}"
                Bk = [[f"ps{g * 4 + i}" for i in range(4)] for g in range(2)]
                Bn = [[ps[g * 4 + i] for i in range(4)] for g in range(2)]
                self.o_ts("dve", glI[:], i8[:], gcr[sl][:, 127:128], ALU.mult, ["i8", gk], ["glI"])
                self.o_mm(ps[2][:, 0:8], self.ones_f[0:8, :], glI[:], True, True, ["ones_f", "glI"], ["ps2"])
                self.o_act(cd[:], ps[2][:, 0:8], AF.Exp, ["ps2"], ["cd"])
                for g in range(2):
                    for hh in range(4):
                        h = g * 4 + hh
                        o_ = Bn[g][0][:, hh * 128:(hh + 1) * 128]
                        self.o_mm(o_, sel[:, h, :], gcr[sl][:], True, False, ["sel", gk], [Bk[g][0]])
                        self.o_mm(o_, gcr[sl][:], seln[:, h, :], False, True, ["seln", gk], [Bk[g][0]])
                for g in range(2):
                    w_ = W[g]
                    self.o_tt("dve", w_["tmpD"][:], v4(Bn[g][0]), triadd[:], ALU.add, [Bk[g][0], "triadd"], [f"g{g}tmpD"])
                    self.o_act(w_["E"][:], w_["tmpD"][:], AF.Exp, [f"g{g}tmpD"], [f"g{g}E"])
                    self.o_tt("pool", w_["Es"][:], w_["E"][:], strict[:], ALU.mult, [f"g{g}E", "strict"], [f"g{g}Es"])
                for g in range(2):
                    for hh in range(4):
                        h = g * 4 + hh
                        self.o_mm(Bn[g][1][:, hh * 128:(hh + 1) * 128], L["gkt"][:, h, :], L["gkb"][:, h, :], True, True,
                                  [lk("gkt"), lk("gkb")], [Bk[g][1]])
                    for hh in range(4):
                        h = g * 4 + hh
                        self.o_mm(Bn[g][2][:, hh * 128:(hh + 1) * 128], L["gkt"][:, h, :], L["gq"][:, h, :], True, True,
                                  [lk("gkt"), lk("gq")], [Bk[g][2]])
                for g in range(2):
                    w_ = W[g]
                    self.o_stt(w_["Na"][:], v4(Bn[g][1]), -1.0, w_["Es"][:], ALU.mult, ALU.mult, [Bk[g][1], f"g{g}Es"], [f"g{g}Na"])
                    self.o_tt("dve", w_["attnT"][:], v4(Bn[g][2]), w_["E"][:], ALU.mult, [Bk[g][2], f"g{g}E"], [f"g{g}attnT"])
                for g in range(2):
                    w_ = W[g]
                    for hh in range(4):
                        self.o_tr(Bn[g][3][:, hh * 128:(hh + 1) * 128], w_["Na"][:, hh, :], self.ident_f[:],
                                  [f"g{g}Na", "ident_f"], [Bk[g][3]])
                    self.o_cp("act", w_["Ma"][:], v4(Bn[g][3]), [Bk[g][3]], [f"g{g}Ma"])
                    self.o_tt("dve", w_["Ua"][:], w_["Na"][:], ident4[:], ALU.add, [f"g{g}Na", "ident4"], [f"g{g}Ua"])
                cur = ["a", "a"]
                for lev in range(1, 7):
                    for g in range(2):
                        w_ = W[g]
                        c_ = cur[g]
                        nx = "b" if c_ == "a" else "a"
                        Np, Mp = w_["N" + c_], w_["M" + c_]
                        Nn, Mn = w_["N" + nx], w_["M" + nx]
                        kNp, kMp = f"g{g}N{c_}", f"g{g}M{c_}"
                        kNn, kMn = f"g{g}N{nx}", f"g{g}M{nx}"
                        if lev < 6:
                            for hh in range(4):
                                self.o_mm(Bn[g][0][:, hh * 128:(hh + 1) * 128], Mp[:, hh, :], Np[:, hh, :], True, True,
                                          [kNp, kMp], [Bk[g][0]])
                        for hh in range(4):
                            self.o_mm(Bn[g][1][:, hh * 128:(hh + 1) * 128], Np[:, hh, :], Mp[:, hh, :], True, True,
                                      [kNp, kMp], [Bk[g][1]])
                        if lev < 6:
                            self.o_cp("act", Nn[:], v4(Bn[g][0]), [Bk[g][0]], [kNn])
                        self.o_cp("dve", Mn[:], v4(Bn[g][1]), [Bk[g][1]], [kMn])
                    for g in range(2):
                        w_ = W[g]
                        c_ = cur[g]
                        nx = "b" if c_ == "a" else "a"
                        Mn = w_["M" + nx]
                        kMn = f"g{g}M{nx}"
                        Up = w_["Ua"] if c_ == "a" else w_["Ub2"]
                        Un = w_["Ub2"] if c_ == "a" else w_["Ua"]
                        kUp = f"g{g}U{c_}"
                        kUn = f"g{g}U{nx}"
                        for hh in range(4):
                            self.o_mm(Bn[g][2][:, hh * 128:(hh + 1) * 128], Mn[:, hh, :], Up[:, hh, :], True, True,
                                      [kMn, kUp], [Bk[g][2]])
                        self.o_tt("dve", Un[:], Up[:], v4(Bn[g][2]), ALU.add, [kUp, Bk[g][2]], [kUn])
                        cur[g] = nx
                for g in range(2):
                    w_ = W[g]
                    Uf = w_["Ua"] if cur[g] == "a" else w_["Ub2"]
                    kU = f"g{g}U{cur[g]}"
                    self.o_cp("act", w_["Ubf"][:], Uf[:], [kU], [f"g{g}Ubf"])
                    for hh in range(4):
                        h = g * 4 + hh
                        self.o_mm(Bn[g][3][:, hh * 128:(hh + 1) * 128], L["gkbg"][:, h, :], w_["Ubf"][:, hh, :], True, True,
                                  [lk("gkbg"), f"g{g}Ubf"], [Bk[g][3]])
                    P.op("act", lambda e, w_=w_, bank=Bn[g][3]: e.mul(out=w_["nwT"][:], in_=v4(bank), mul=-1.0),
                         reads=[Bk[g][3]], writes=[f"g{g}nwT"])
                for g in range(2):
                    w_ = W[g]
                    for hh in range(4):
                        h = g * 4 + hh
                        o_ = Bn[g][0][:, hh * 128:(hh + 1) * 128]
                        self.o_mm(o_, w_["Ubf"][:, hh, :], L["gvb"][:, h, :], True, False, [f"g{g}Ubf", lk("gvb")], [Bk[g][0]])
                        self.o_mm(o_, w_["nwT"][:, hh, :], Sb[:, h, :], False, True, [f"g{g}nwT", f"Sb{g}"], [Bk[g][0]])
                    self.o_cp("act", w_["vn"][:], v4(Bn[g][0]), [Bk[g][0]], [f"g{g}vn"])
                for g in range(2):
                    w_ = W[g]
                    for hh in range(4):
                        h = g * 4 + hh
                        o_ = Bn[g][1][:, hh * 128:(hh + 1) * 128]
                        self.o_mm(o_, Sb[:, h, :], L["gqg"][:, h, :], True, False, [f"Sb{g}", lk("gqg")], [Bk[g][1]])
                        self.o_mm(o_, w_["vn"][:, hh, :], w_["attnT"][:, hh, :], False, True, [f"g{g}vn", f"g{g}attnT"], [Bk[g][1]])
                    for hh in range(4):
                        h = g * 4 + hh
                        self.o_mm(Bn[g][2][:, hh * 128:(hh + 1) * 128], L["gkd"][:, h, :], w_["vn"][:, hh, :], True, True,
                                  [lk("gkd"), f"g{g}vn"], [Bk[g][2]])
                for g in range(2):
                    w_ = W[g]
                    for hh in range(4):
                        h = g * 4 + hh
                        self.o_stt(Sf[:, h, :], Sf[:, h, :], cd[:, h:h + 1], Bn[g][2][:, hh * 128:(hh + 1) * 128],
                                   ALU.mult, ALU.add, [f"Sf{g}", "cd", Bk[g][2]], [f"Sf{g}"])
                    self.o_cp("act", Sb[:, g * 4:(g + 1) * 4, :], Sf[:, g * 4:(g + 1) * 4, :], [f"Sf{g}"], [f"Sb{g}"])
                for g in range(2):
                    w_ = W[g]
                    self.o_act(w_["sq"][:], v4(Bn[g][1]), AF.Square, [Bk[g][1]], [f"g{g}sq"])
                    self.o_mm(Bn[g][3][:], self.ones_f[:], w_["sq"][:].rearrange("p h c -> p (h c)"), True, True,
                              ["ones_f", f"g{g}sq"], [Bk[g][3]])
                    self.o_act(w_["rstd"][:], v4(Bn[g][3]), AF.Ln, [Bk[g][3], "epsn"], [f"g{g}rstd"], bias=epsn[:], scale=1.0 / 128.0)
                    self.o_act(w_["sq"][:], w_["rstd"][:], AF.Exp, [f"g{g}rstd"], [f"g{g}sq"], scale=-0.5)
                    self.o_stt(w_["tt"][:], v4(Bn[g][1]), gn[:], w_["sq"][:], ALU.mult, ALU.mult,
                               [Bk[g][1], "gn", f"g{g}sq"], [f"g{g}tt"])
                    osl = (n // 4) % 2
                    c4 = n % 4
                    self.o_tt("pool", ogst[osl][:, g * 4:(g + 1) * 4, c4 * 128:(c4 + 1) * 128], w_["tt"][:],
                              ldsz[sl][:, g * 4:(g + 1) * 4, :], ALU.mult, [f"g{g}tt", f"ld{sl}_gsz"], [f"ogst{osl}"])
                if n % 4 == 3:
                    osl = (n // 4) % 2
                    t0 = (n // 4) * 512
                    P.dma("sp", f"ogst{osl}", self.OT[:, :, t0:t0 + 512].rearrange("h p t -> p h t"), ogst[osl][:],
                          reads=[f"ogst{osl}"])
            self.end_phase()

    def phase_copy_out(self, x_src):
        P = self.P
        T = self.T
        n = max(1, T // 2048)
        rows = T // n
        for i in range(n):
            P.dma("sp", "fin", self.y_out[i * rows:(i + 1) * rows, :], x_src[i * rows:(i + 1) * rows, :])
        self.end_phase()

    def build(self):
        self.phase_setup()
        plan = self.plan if self.plan is not None else full_plan()
        cur = self.x_in
        bufs = [self.xa, self.xb]
        nb = 0
        for step in plan:
            kind = step[0]
            if kind == "ffn":
                _, which, l = step
                w_in = (self.ffn1_w_in if which == 1 else self.ffn2_w_in)[l]
                w_out = (self.ffn1_w_out if which == 1 else self.ffn2_w_out)[l]
                dst = bufs[nb]
                nb ^= 1
                self.phase_ffn(cur, dst, w_in, w_out, l, 0 if which == 1 else 2)
                cur = dst
            elif kind == "kv":
                self.phase_proj_fm(cur, self.diff_w_kv, 0, 8, self.KT, None, also_v=1024)
            elif kind == "attn":
                _, l = step
                jl = l - N_A
                lambda_init = 0.8 - 0.6 * math.exp(-0.3 * l)
                self.phase_proj_fm(cur, self.diff_w_q[jl], 0, 8, self.QT, 0.125)
                self.phase_attn(jl, lambda_init)
                dst = bufs[nb]
                nb ^= 1
                self.phase_outproj(cur, dst, self.diff_w_out[jl], self.OT, l)
                cur = dst
            elif kind == "gdn":
                _, l = step
                self.phase_gdn_proj(cur, l)
                self.phase_gdn_core(l)
                dst = bufs[nb]
                nb ^= 1
                self.phase_outproj(cur, dst, self.gdn_w_out[l], self.OT, l)
                cur = dst
        self.phase_copy_out(cur)
        return self.nc


def full_plan():
    plan = []
    for l in range(DEPTH):
        plan.append(("ffn", 1, l))
        plan.append(("gdn", l) if l < N_A else ("attn", l))
        plan.append(("ffn", 2, l))
        if l == N_A - 1:
            plan.append(("kv",))
    return plan


_CACHE = {}


def get_program(T, plan=None, debug=False):
    key = (T, repr(plan), debug)
    if key not in _CACHE:
        _CACHE[key] = Builder(T, None, debug=debug, plan=plan).build()
    return _CACHE[key]


WEIGHT_NAMES = ["ln_g", "ln_b", "ffn1_w_in", "ffn1_w_out", "ffn2_w_in", "ffn2_w_out", "gdn_w_in",
                "gdn_conv_w", "gdn_a_log", "gdn_dt_bias", "gdn_norm_g", "gdn_w_out", "diff_w_kv",
                "diff_lambda_k", "diff_w_q", "diff_lambda_q", "diff_norm_g", "diff_w_out"]


def run(inputs, plan=None, trace=False, debug=False):
    x = np.ascontiguousarray(np.asarray(inputs["x"], dtype=np.float32))
    B, T, _ = x.shape
    assert B == NCORES
    nc = get_program(T, plan, debug)
    consts = make_consts()
    shared = {k: np.ascontiguousarray(np.asarray(inputs[k], dtype=np.float32)) for k in WEIGHT_NAMES}
    in_maps = []
    for b in range(B):
        m = {"x": x[b], "consts": consts}
        m.update(shared)
        in_maps.append(m)
    res = run_bass_kernel_spmd(nc, in_maps, core_ids=list(range(NCORES)), trace=trace)
    out = np.stack([res.results[b]["y"] for b in range(B)], axis=0)
    return out, res


def kernel(**inputs):
    out, _ = run(inputs)
    return out
```

```python
from contextlib import ExitStack
import math
import numpy as np
import concourse.bass as bass
import concourse.mybir as mybir
from concourse.bass_utils import run_bass_kernel_spmd

F32 = mybir.dt.float32
BF16 = mybir.dt.bfloat16
AF = mybir.ActivationFunctionType
ALU = mybir.AluOpType

D = 1024
DFF = 2816
DEPTH = 4
N_A = 2
ALPHA = (2.0 * DEPTH) ** 0.25
LN_EPS = 1e-5
NCORES = 8
SEQ = 8192
NEG = -1.0e30


class Op:
    __slots__ = ("eng", "fn", "deps", "sig", "idx", "dsem", "dval")


class Prog:
    CE = ("pe", "act", "dve", "pool")
    ALLE = ("pe", "act", "dve", "pool", "sp")
    BLK = {"pe": "tensor", "act": "scalar", "dve": "vector", "pool": "gpsimd", "sp": "sync"}

    def __init__(self, nc):
        self.nc = nc
        self.esem = {e: nc.alloc_semaphore("sem_" + e) for e in self.CE}
        self.ecnt = {e: 0 for e in self.CE}
        self.dsems = {}
        self.waited = {e: {} for e in self.ALLE}
        self.nops = 0
        self.reset()

    def reset(self):
        self.ops = []
        self.last_w = {}
        self.readers = {}
        self.phase_dma = {}

    def _collect(self, reads, writes):
        deps = []
        for k in reads:
            for o in self.last_w.get(k, ()):
                deps.append(o)
        for k in writes:
            for o in self.last_w.get(k, ()):
                deps.append(o)
            for o in self.readers.get(k, ()):
                deps.append(o)
        return deps

    def _record(self, o, reads, writes):
        isdma = o.dsem is not None
        for k in reads:
            lst = self.readers.setdefault(k, [])
            if not isdma:
                lst[:] = [p for p in lst if not (p.dsem is None and p.eng == o.eng)]
            lst.append(o)
        for k in writes:
            self.readers[k] = []
            self.last_w[k] = [o]

    def op(self, eng, fn, reads=(), writes=(), dsem=None, dval=0, nodeps=False):
        o = Op()
        o.eng = eng
        o.fn = fn
        o.sig = False
        o.idx = 0
        o.dsem = dsem
        o.dval = dval
        deps = [] if nodeps else self._collect(reads, writes)
        o.deps = []
        seen = set()
        for d in deps:
            if id(d) in seen or d is o:
                continue
            seen.add(id(d))
            if d.dsem is None:
                if d.eng == "pe" and eng == "pe":
                    continue
                d.sig = True
            o.deps.append(d)
        self._record(o, reads, writes)
        self.ops.append(o)
        return o

    def dma(self, q, semkey, out, in_, reads=(), writes=(), nodeps=False):
        if semkey not in self.dsems:
            self.dsems[semkey] = [self.nc.alloc_semaphore("d_" + semkey), 0]
        ent = self.dsems[semkey]
        ent[1] += 16
        o = self.op(q, lambda e, out=out, in_=in_: e.dma_start(out=out, in_=in_), reads, writes,
                    dsem=ent[0], dval=ent[1], nodeps=nodeps)
        self.phase_dma[(q, semkey)] = (ent[0], ent[1])
        return o

    def flush(self):
        for o in self.ops:
            if o.dsem is None and o.sig:
                self.ecnt[o.eng] += 1
                o.idx = self.ecnt[o.eng]
        per = {e: [] for e in self.ALLE}
        for o in self.ops:
            per[o.eng].append(o)
        self.nops += len(self.ops)
        with self.nc.Block() as blk:
            for e in self.ALLE:
                def body(eng, e=e, ops=per[e]):
                    waited = self.waited[e]
                    for o in ops:
                        for d in o.deps:
                            if d.dsem is not None:
                                sem, val, key = d.dsem, d.dval, id(d.dsem)
                            else:
                                sem, val, key = self.esem[d.eng], d.idx, d.eng
                            if waited.get(key, 0) < val:
                                eng.wait_ge(sem, val)
                                waited[key] = val
                        ins = o.fn(eng)
                        if o.dsem is not None:
                            ins.then_inc(o.dsem, 16)
                        elif o.sig:
                            ins.then_inc(self.esem[e], 1)
                    for (q, _k), (sem, val) in self.phase_dma.items():
                        if q == e and waited.get(id(sem), 0) < val:
                            eng.wait_ge(sem, val)
                            waited[id(sem)] = val
                getattr(blk, self.BLK[e])(body)
        self.reset()


C_IDENT = 0
C_ONES = 128
C_TRIADD4 = 256
C_STRICT4 = 768
C_IDENT4 = 1280
C_AMASK = 1792
C_SEL = 3840
C_SCAN = 4864
C_I8 = 5376
C_SGN = 5384
C_COLS = 5512


def make_consts():
    c = np.zeros((128, C_COLS), np.float32)
    c[:, C_IDENT:C_IDENT + 128] = np.eye(128, dtype=np.float32)
    c[:, C_ONES:C_ONES + 128] = 1.0
    s = np.arange(128)[:, None]
    cc = np.arange(128)[None, :]
    tri = np.where(s <= cc, 0.0, NEG).astype(np.float32)
    strict = (s < cc).astype(np.float32)
    for r in range(4):
        c[:, C_TRIADD4 + r * 128:C_TRIADD4 + (r + 1) * 128] = tri
        c[:, C_STRICT4 + r * 128:C_STRICT4 + (r + 1) * 128] = strict
        c[:, C_IDENT4 + r * 128:C_IDENT4 + (r + 1) * 128] = np.eye(128, dtype=np.float32)
        q = np.arange(512)[None, :]
        c[:, C_AMASK + r * 512:C_AMASK + (r + 1) * 512] = ((r * 128 + s) <= q).astype(np.float32)
    for h in range(8):
        c[h, C_SEL + h * 128:C_SEL + (h + 1) * 128] = 1.0
    t = np.arange(512)
    c[0:8, C_SCAN:C_SCAN + 512] = (t % 128 != 0).astype(np.float32)[None, :]
    c[0:8, C_I8:C_I8 + 8] = np.eye(8, dtype=np.float32)
    c[0, C_SGN:C_SGN + 128] = -1.0
    c[1, C_SGN:C_SGN + 128] = 1.0
    return c


class Builder:
    def __init__(self, T, nphases=None, debug=False, plan=None):
        self.plan = plan
        self.T = T
        self.NST = T // 512
        self.nphases_limit = nphases
        self.nc = bass.Bass("TRN2", target_bir_lowering=False)
        nc = self.nc
        dt = lambda name, shape, dtype, kind: nc.dram_tensor(name, shape, dtype, kind=kind).ap()
        self.x_in = dt("x", [T, D], F32, "ExternalInput")
        self.consts_d = dt("consts", [128, C_COLS], F32, "ExternalInput")
        self.ln_g = dt("ln_g", [DEPTH, 3, D], F32, "ExternalInput")
        self.ln_b = dt("ln_b", [DEPTH, 3, D], F32, "ExternalInput")
        self.ffn1_w_in = dt("ffn1_w_in", [DEPTH, D, 2 * DFF], F32, "ExternalInput")
        self.ffn1_w_out = dt("ffn1_w_out", [DEPTH, DFF, D], F32, "ExternalInput")
        self.ffn2_w_in = dt("ffn2_w_in", [DEPTH, D, 2 * DFF], F32, "ExternalInput")
        self.ffn2_w_out = dt("ffn2_w_out", [DEPTH, DFF, D], F32, "ExternalInput")
        self.gdn_w_in = dt("gdn_w_in", [N_A, D, 4112], F32, "ExternalInput")
        self.gdn_conv_w = dt("gdn_conv_w", [N_A, 4, 3072], F32, "ExternalInput")
        self.gdn_a_log = dt("gdn_a_log", [N_A, 8], F32, "ExternalInput")
        self.gdn_dt_bias = dt("gdn_dt_bias", [N_A, 8], F32, "ExternalInput")
        self.gdn_norm_g = dt("gdn_norm_g", [N_A, 128], F32, "ExternalInput")
        self.gdn_w_out = dt("gdn_w_out", [N_A, D, D], F32, "ExternalInput")
        self.diff_w_kv = dt("diff_w_kv", [D, 2 * D], F32, "ExternalInput")
        self.diff_lambda_k = dt("diff_lambda_k", [2, 64], F32, "ExternalInput")
        self.diff_w_q = dt("diff_w_q", [2, D, D], F32, "ExternalInput")
        self.diff_lambda_q = dt("diff_lambda_q", [2, 2, 64], F32, "ExternalInput")
        self.diff_norm_g = dt("diff_norm_g", [2, 128], F32, "ExternalInput")
        self.diff_w_out = dt("diff_w_out", [2, D, D], F32, "ExternalInput")
        self.y_out = dt("y", [T, D], F32, "ExternalOutput")
        self.xa = dt("xa", [T, D], F32, "Internal")
        self.xb = dt("xb", [T, D], F32, "Internal")
        skind = "ExternalOutput" if debug else "Internal"
        self.KT = dt("KT", [8, 128, T], BF16, skind)
        self.VD = dt("VD", [8, 128, T], BF16, skind)
        self.QT = dt("QT", [8, 128, T], BF16, skind)
        self.OT = dt("OT", [8, 128, T], BF16, skind)
        NCH = T // 128
        self.G = {}
        for nm in ("gq", "gqg", "gkt", "gkb", "gkbg", "gkd", "gvb"):
            self.G[nm] = dt(nm, [NCH, 128, 1024], BF16, skind)
        self.G["gsz"] = dt("gsz", [NCH, 128, 1024], F32, skind)
        self.GGC = dt("ggc", [8, T], F32, skind)
        self.P = Prog(nc)
        self.nphase = 0
        self.stack = ExitStack()
        sb = lambda name, shape, dtype: nc.alloc_sbuf_tensor(name, shape, dtype)
        self.ident_f = sb("ident_f", [128, 128], F32)
        self.ident_b = sb("ident_b", [128, 128], BF16)
        self.ones_f = sb("ones_f", [128, 128], F32)
        self.ones_b = sb("ones_b", [128, 128], BF16)
        self.epsln = sb("epsln", [128, 1], F32)
        self.ps = [nc.alloc_psum_tensor(f"ps{i}", [128, 512], F32) for i in range(8)]

    def more(self):
        return self.nphases_limit is None or self.nphase < self.nphases_limit

    def end_phase(self):
        self.P.flush()
        self.nphase += 1

    def phase_setup(self):
        P = self.P
        cd = self.consts_d
        P.dma("sp", "c0", self.ident_f[:], cd[:, C_IDENT:C_IDENT + 128], writes=["ident_f"])
        P.dma("sp", "c1", self.ones_f[:], cd[:, C_ONES:C_ONES + 128], writes=["ones_f"])
        P.op("dve", lambda e: e.tensor_copy(out=self.ident_b[:], in_=self.ident_f[:]),
             reads=["ident_f"], writes=["ident_b"])
        P.op("dve", lambda e: e.tensor_copy(out=self.ones_b[:], in_=self.ones_f[:]),
             reads=["ones_f"], writes=["ones_b"])
        P.op("dve", lambda e: e.memset(self.epsln[:], LN_EPS / (ALPHA * ALPHA)), writes=["epsln"])
        self.end_phase()

    def alloc_ln(self, st, A, l, j, nz=1):
        nc = self.nc
        S = lambda name, shape, dtype: st.enter_context(nc.sbuf_tensor(f"{name}_p{self.nphase}", shape, dtype))
        A["z"] = [S(f"z{i}", [128, D], F32) for i in range(nz)]
        A["o"] = [S("o0", [128, D], F32), S("o1", [128, D], F32)]
        A["st6"] = S("st6", [128, 12], F32)
        A["mv"] = S("mv", [128, 2], F32)
        A["sd"] = S("sd", [128, 1], F32)
        A["rstd"] = S("rstd", [128, 1], F32)
        A["nmr"] = S("nmr", [128, 1], F32)
        A["gbc"] = S("gbc", [128, D], F32)
        A["bbc"] = S("bbc", [128, D], F32)
        self.P.dma("sp", "lng", A["gbc"][:], self.ln_g[l, j:j + 1, :].to_broadcast([128, D]), writes=["gbc"])
        self.P.dma("sp", "lnb", A["bbc"][:], self.ln_b[l, j:j + 1, :].to_broadcast([128, D]), writes=["bbc"])

    def load_x(self, x_src, t0, xinb):
        self.P.dma("pool", "xinb", xinb[:], x_src[t0:t0 + 512, :].rearrange("(s p) d -> p s d", p=128),
                   writes=["xinb"])

    def emit_xT(self, x_src, t0, xinb, xT, prefetch=True):
        P, ps = self.P, self.ps
        if t0 == 0:
            self.load_x(x_src, 0, xinb)
        for kc in range(8):
            bank = ps[kc % 4]
            pb = bank.bitcast(BF16)
            for s in range(4):
                P.op("pe", lambda e, pb=pb, s=s, kc=kc: e.transpose(
                    out=pb[:, s * 128:(s + 1) * 128], in_=xinb[:, s, kc * 128:(kc + 1) * 128],
                    identity=self.ident_b[:]),
                    reads=["xinb", "ident_b"], writes=[f"ps{kc % 4}"])
            if kc % 2 == 0:
                P.op("dve", lambda e, pb=pb, kc=kc: e.tensor_copy(out=xT[:, kc, :], in_=pb[:, 0:512]),
                     reads=[f"ps{kc % 4}"], writes=[f"xT{kc}"])
            else:
                P.op("act", lambda e, pb=pb, kc=kc: e.copy(out=xT[:, kc, :], in_=pb[:, 0:512]),
                     reads=[f"ps{kc % 4}"], writes=[f"xT{kc}"])
        if prefetch and t0 + 512 < self.T:
            self.load_x(x_src, t0 + 512, xinb)

    def evac(self, i, out_ap, in_ap, reads, writes, scale=None):
        P = self.P
        if i % 2 == 0:
            if scale is None:
                P.op("dve", lambda e: e.tensor_copy(out=out_ap, in_=in_ap), reads=reads, writes=writes)
            else:
                P.op("dve", lambda e: e.tensor_scalar(out=out_ap, in0=in_ap, scalar1=scale, scalar2=None,
                                                      op0=ALU.mult), reads=reads, writes=writes)
        else:
            if scale is None:
                P.op("act", lambda e: e.copy(out=out_ap, in_=in_ap), reads=reads, writes=writes)
            else:
                P.op("act", lambda e: e.mul(out=out_ap, in_=in_ap, mul=scale), reads=reads, writes=writes)

    def phase_ffn(self, x_src, x_dst, w_in_d, w_out_d, l, j):
        nc, P = self.nc, self.P
        NST = self.NST
        with ExitStack() as st:
            S = lambda name, shape, dtype: st.enter_context(nc.sbuf_tensor(f"{name}_p{self.nphase}", shape, dtype))
            A = {}
            w1 = S("w1", [128, 8, 2 * DFF], BF16)
            w2 = S("w2", [128, 22, D], BF16)
            xinb = S("xinb", [128, 4, D], BF16)
            xres = [S("xres0", [128, D], F32), S("xres1", [128, D], F32)]
            xT = S("xT", [128, 8, 512], BF16)
            act = S("act", [128, 22, 512], BF16)
            sg = [S("sg0", [128, 512], F32), S("sg1", [128, 512], F32)]
            self.alloc_ln(st, A, l, j)
            ps = self.ps
            Y = [None, None]
            w_in_v = w_in_d.rearrange("(c p) f -> p c f", p=128)
            for cg in range(11):
                c0 = cg * 256
                P.dma("pool", f"wl{cg}", w1[:, :, c0:c0 + 256], w_in_v[:, :, c0:c0 + 256], writes=[f"w1g{cg}"])
                P.dma("pool", f"wl{cg}", w1[:, :, DFF + c0:DFF + c0 + 256], w_in_v[:, :, DFF + c0:DFF + c0 + 256],
                      writes=[f"w1g{cg}", f"w1u{cg}"], nodeps=True)
            w_out_v = w_out_d.rearrange("(c p) f -> p c f", p=128)
            for g in range(2):
                P.dma("pool", f"wl{11 + g}", w2[:, g * 11:(g + 1) * 11, :], w_out_v[:, g * 11:(g + 1) * 11, :],
                      writes=[f"w2_{g}"])
            cmul = 0.5 / ALPHA
            self.emit_xT(x_src, 0, xinb, xT)
            for t in range(NST):
                t0 = t * 512
                for jf in range(22):
                    G = ps[(jf % 2) * 2]
                    U = ps[(jf % 2) * 2 + 1]
                    gk, uk = f"ps{(jf % 2) * 2}", f"ps{(jf % 2) * 2 + 1}"
                    for kc in range(8):
                        P.op("pe", lambda e, G=G, kc=kc, jf=jf: e.matmul(
                            G[:], lhsT=w1[:, kc, jf * 128:(jf + 1) * 128], rhs=xT[:, kc, :],
                            start=(kc == 0), stop=(kc == 7)),
                            reads=[f"w1g{jf // 2}", f"xT{kc}"], writes=[gk])
                    for kc in range(8):
                        P.op("pe", lambda e, U=U, kc=kc, jf=jf: e.matmul(
                            U[:], lhsT=w1[:, kc, DFF + jf * 128:DFF + (jf + 1) * 128], rhs=xT[:, kc, :],
                            start=(kc == 0), stop=(kc == 7)),
                            reads=[f"w1u{jf // 2}", f"xT{kc}"], writes=[uk])
                    sgt = sg[jf % 2]
                    P.op("act", lambda e, G=G, sgt=sgt: e.activation(out=sgt[:], in_=G[:], func=AF.Silu),
                         reads=[gk], writes=[f"sg{jf % 2}"])
                    P.op("dve", lambda e, U=U, sgt=sgt, jf=jf: e.tensor_tensor(
                        out=act[:, jf, :], in0=sgt[:], in1=U[:], op=ALU.mult),
                        reads=[uk, f"sg{jf % 2}"], writes=[f"act{jf}"])
                for s in range(4):
                    slot = s % 2
                    r0 = t0 + s * 128
                    P.dma("sp", f"xres{slot}", xres[slot][:], x_src[r0:r0 + 128, :], writes=[f"xres{slot}"])
                    ya, yb = ps[4 + slot * 2], ps[5 + slot * 2]
                    yk = [f"ps{4 + slot * 2}", f"ps{5 + slot * 2}"]
                    for half, yp in enumerate((ya, yb)):
                        for jf in range(22):
                            P.op("pe", lambda e, yp=yp, jf=jf, s=s, half=half: e.matmul(
                                yp[:], lhsT=act[:, jf, s * 128:(s + 1) * 128],
                                rhs=w2[:, jf, half * 512:(half + 1) * 512],
                                start=(jf == 0), stop=(jf == 21)),
                                reads=[f"act{jf}", f"w2_{jf // 11}"], writes=[yk[half]])
                    self.ln_epilogue_2bank(A, ya, yb, yk, xres[slot], f"xres{slot}", cmul, slot,
                                           x_dst[r0:r0 + 128, :], "st")
                    if s == 1 and t + 1 < NST:
                        self.emit_xT(x_src, t0 + 512, xinb, xT)
            self.end_phase()

    def ln_epilogue_2bank(self, A, ya, yb, yk, xres, xres_key, cmul, slot, out_dram, semkey):
        P = self.P
        zi = slot % len(A["z"])
        z = A["z"][zi]
        o = A["o"][slot]
        st6, mv, sd, rstd, nmr = A["st6"], A["mv"], A["sd"], A["rstd"], A["nmr"]
        gbc, bbc = A["gbc"], A["bbc"]
        for half, yp in enumerate((ya, yb)):
            sl = slice(half * 512, (half + 1) * 512)
            P.op("dve", lambda e, yp=yp, sl=sl: e.scalar_tensor_tensor(
                out=z[:, sl], in0=yp[:], scalar=cmul, in1=xres[:, sl], op0=ALU.mult, op1=ALU.add),
                reads=[yk[half], xres_key], writes=[f"z{zi}_{half}"])
            P.op("dve", lambda e, sl=sl, half=half: e.bn_stats(out=st6[:, half * 6:(half + 1) * 6], in_=z[:, sl]),
                 reads=[f"z{zi}_{half}"], writes=[f"st6{half}"])
        P.op("dve", lambda e: e.bn_aggr(out=mv[:], in_=st6[:]), reads=["st60", "st61"], writes=["mv"])
        P.op("act", lambda e: e.activation(out=sd[:], in_=mv[:, 1:2], func=AF.Sqrt, bias=self.epsln[:], scale=1.0),
             reads=["mv", "epsln"], writes=["sd"])
        P.op("dve", lambda e: e.reciprocal(out=rstd[:], in_=sd[:]), reads=["sd"], writes=["rstd"])
        P.op("dve", lambda e: e.scalar_tensor_tensor(out=nmr[:], in0=mv[:, 0:1], scalar=-1.0, in1=rstd[:],
                                                     op0=ALU.mult, op1=ALU.mult),
             reads=["mv", "rstd"], writes=["nmr"])
        ok = f"o{slot}"
        P.op("act", lambda e: e.activation(out=o[:], in_=z[:], func=AF.Identity, bias=nmr[:], scale=rstd[:]),
             reads=[f"z{zi}_0", f"z{zi}_1", "rstd", "nmr"], writes=[ok])
        P.op("pool", lambda e: e.tensor_tensor(out=o[:], in0=o[:], in1=gbc[:], op=ALU.mult),
             reads=[ok, "gbc"], writes=[ok])
        P.op("pool", lambda e: e.tensor_tensor(out=o[:], in0=o[:], in1=bbc[:], op=ALU.add),
             reads=[ok, "bbc"], writes=[ok])
        P.dma("sp", f"{semkey}{slot}", out_dram, o[:], reads=[ok])

    def phase_proj_fm(self, x_src, w_d, col0, ncols_chunks, dst, scale, also_v=None):
        nc, P, ps = self.nc, self.P, self.ps
        NST = self.NST
        wcols = w_d.shape[1]
        with ExitStack() as st:
            S = lambda name, shape, dtype: st.enter_context(nc.sbuf_tensor(f"{name}_p{self.nphase}", shape, dtype))
            w = S("wp", [128, 8, wcols], BF16)
            xinb = S("xinb", [128, 4, D], BF16)
            xT = S("xT", [128, 8, 512], BF16)
            kst = [S("kst0", [128, 8, 512], BF16), S("kst1", [128, 8, 512], BF16)]
            if also_v is not None:
                vst = [S("vst0", [128, 8, 4, 128], BF16), S("vst1", [128, 8, 4, 128], BF16)]
            w_v = w_d.rearrange("(c p) f -> p c f", p=128)
            for kc in range(8):
                P.dma("pool", f"wl{kc}", w[:, kc, :], w_v[:, kc, :], writes=[f"wp{kc}"])
            ev = 0
            for t in range(NST):
                t0 = t * 512
                slot = t % 2
                self.emit_xT(x_src, t0, xinb, xT)
                for hc in range(ncols_chunks):
                    b = 4 + hc % 4
                    bank = ps[b]
                    for kc in range(8):
                        P.op("pe", lambda e, bank=bank, kc=kc, hc=hc: e.matmul(
                            bank[:], lhsT=w[:, kc, col0 + hc * 128:col0 + (hc + 1) * 128], rhs=xT[:, kc, :],
                            start=(kc == 0), stop=(kc == 7)),
                            reads=[f"wp{kc}", f"xT{kc}"], writes=[f"ps{b}"])
                    self.evac(ev, kst[slot][:, hc, :], bank[:], [f"ps{b}"], [f"kst{slot}"], scale=scale)
                    ev += 1
                P.dma("sp", f"kst{slot}", dst[:, :, t0:t0 + 512].rearrange("h p t -> p h t"), kst[slot][:],
                      reads=[f"kst{slot}"])
                if also_v is not None:
                    vc0 = also_v
                    for s_ in range(4):
                        for half in range(2):
                            b = 4 + (s_ * 2 + half) % 4
                            bank = ps[b]
                            for kc in range(8):
                                P.op("pe", lambda e, bank=bank, kc=kc, s_=s_, half=half: e.matmul(
                                    bank[:], lhsT=xT[:, kc, s_ * 128:(s_ + 1) * 128],
                                    rhs=w[:, kc, vc0 + half * 512:vc0 + (half + 1) * 512],
                                    start=(kc == 0), stop=(kc == 7)),
                                    reads=[f"wp{kc}", f"xT{kc}"], writes=[f"ps{b}"])
                            self.evac(ev, vst[slot][:, half * 4:(half + 1) * 4, s_, :],
                                      bank[:].rearrange("p (h v) -> p h v", h=4),
                                      [f"ps{b}"], [f"vst{slot}"])
                            ev += 1
                    P.dma("sp", f"vst{slot}", self.VD[:, :, t0:t0 + 512].rearrange("h p f -> p h f"),
                          vst[slot][:].rearrange("p h s v -> p h (s v)"), reads=[f"vst{slot}"])
            self.end_phase()

    def phase_attn(self, jl, lambda_init):
        nc, P, ps = self.nc, self.P, self.ps
        T = self.T
        NQ = T // 512
        X = mybir.AxisListType.X
        with ExitStack() as st:
            S = lambda name, shape, dtype: st.enter_context(nc.sbuf_tensor(f"{name}_p{self.nphase}", shape, dtype))
            kT = S("kT", [128, T], BF16)
            vv = S("vv", [128, T], BF16)
            qT = S("qT", [128, T], BF16)
            oT = S("oT", [128, T], BF16)
            PT = [[S(f"pt{c}{k}", [128, 512], BF16) for k in range(3)] for c in range(2)]
            lacc = [S("lacc0", [128, 512], F32), S("lacc1", [128, 512], F32)]
            amask = S("amask", [128, 4, 512], BF16)
            tmp = {k: S("e_" + k, [128, 512], F32) for k in ("r0", "r1", "a", "b", "c", "d", "e", "f", "g", "h")}
            lq = S("lq", [2, 64], F32)
            lk = S("lk", [2, 64], F32)
            lprod = S("lprod", [2, 64], F32)
            lsum = S("lsum", [2, 1], F32)
            lexp = S("lexp", [2, 1], F32)
            sgn = S("sgn", [2, 128], F32)
            nlam = S("nlam", [128, 1], F32)
            graw = S("graw", [128, 1], F32)
            gs = S("gs", [128, 1], F32)
            epsn = S("epsn", [128, 1], F32)
            P.dma("pool", "amask", amask[:], self.consts_d[:, C_AMASK:C_AMASK + 2048].rearrange("p (r q) -> p r q", r=4),
                  writes=["amask"])
            P.dma("sp", "lq", lq[:], self.diff_lambda_q[jl], writes=["lq"])
            P.dma("sp", "lk", lk[:], self.diff_lambda_k, writes=["lk"])
            P.dma("sp", "sgn", sgn[:], self.consts_d[0:2, C_SGN:C_SGN + 128], writes=["sgn"])
            P.dma("sp", "graw", graw[:], self.diff_norm_g[jl].rearrange("(p o) -> p o", o=1), writes=["graw"])
            P.op("dve", lambda e: e.memset(epsn[:], 1e-5), writes=["epsn"])
            P.op("dve", lambda e: e.tensor_tensor(out=lprod[:], in0=lq[:], in1=lk[:], op=ALU.mult),
                 reads=["lq", "lk"], writes=["lprod"])
            P.op("dve", lambda e: e.tensor_reduce(out=lsum[:], in_=lprod[:], axis=X, op=ALU.add),
                 reads=["lprod"], writes=["lsum"])
            P.op("act", lambda e: e.activation(out=lexp[:], in_=lsum[:], func=AF.Exp), reads=["lsum"], writes=["lexp"])
            P.op("pe", lambda e: e.matmul(ps[0][:, 0:1], lhsT=sgn[:], rhs=lexp[:], start=True, stop=True),
                 reads=["sgn", "lexp"], writes=["ps0"])
            P.op("dve", lambda e: e.tensor_scalar(out=nlam[:], in0=ps[0][:, 0:1], scalar1=-float(lambda_init),
                                                  scalar2=None, op0=ALU.add), reads=["ps0"], writes=["nlam"])
            P.op("dve", lambda e: e.tensor_scalar(out=gs[:], in0=graw[:], scalar1=float(1.0 - lambda_init),
                                                  scalar2=None, op0=ALU.mult), reads=["graw"], writes=["gs"])
            pending = []
            for h in range(8):
                P.dma("sp", "kT", kT[:], self.KT[h], writes=["kT"])
                P.dma("sp", "vv", vv[:], self.VD[h], writes=["vv"])
                P.dma("sp", "qT", qT[:], self.QT[h], writes=["qT"])
                for i in range(NQ):
                    nj = 4 * i + 4
                    qsl = slice(i * 512, (i + 1) * 512)

                    def qk(j, i=i, qsl=qsl):
                        for c in range(2):
                            b = (j % 2) if c == 0 else (2 + j % 3)
                            bank = ps[b]
                            pt = PT[c][j % 3]
                            pk = f"pt{c}{j % 3}"
                            self.o_mm(bank[:], kT[c * 64:(c + 1) * 64, j * 128:(j + 1) * 128],
                                      qT[c * 64:(c + 1) * 64, qsl], True, True, ["kT", "qT"], [f"ps{b}"])
                            self.o_act(pt[:], bank[:], AF.Exp, [f"ps{b}"], [pk])
                            if j >= 4 * i:
                                r = j - 4 * i
                                self.o_tt("dve", pt[:], pt[:], amask[:, r, :], ALU.mult, [pk, "amask"], [pk])
                            if c == 1:
                                if j == 0:
                                    self.o_cp("dve", lacc[c][:], pt[:], [pk], [f"lacc{c}"])
                                else:
                                    self.o_tt("dve", lacc[c][:], lacc[c][:], pt[:], ALU.add, [pk, f"lacc{c}"], [f"lacc{c}"])

                    def pv(j, nj=nj):
                        for c in range(2):
                            pt = PT[c][j % 3]
                            pk = f"pt{c}{j % 3}"
                            self.o_mm(ps[6 + c][:], vv[:, j * 128:(j + 1) * 128], pt[:], j == 0, j == nj - 1,
                                      ["vv", pk], [f"ps{6 + c}"])
                            if c == 0:
                                self.o_mm(ps[5][:], self.ones_b[:], pt[:], j == 0, j == nj - 1, ["ones_b", pk], ["ps5"])

                    qk(0)
                    if nj > 1:
                        qk(1)
                    for j in range(nj):
                        if j + 2 < nj:
                            qk(j + 2)
                        pv(j)
                        if j == min(8, nj - 1):
                            while pending:
                                pending.pop(0)()
                    t_ = tmp
                    self.o_cp("dve", t_["a"][:], ps[6][:], ["ps6"], ["e_a"])
                    self.o_cp("dve", t_["b"][:], ps[7][:], ["ps7"], ["e_b"])

                    self.o_mm(ps[3][:], self.ones_f[:], lacc[1][:], True, True, ["ones_f", "lacc1"], ["ps3"])
                    self.o_cp("dve", t_["g"][:], ps[5][:], ["ps5"], ["e_g"])
                    self.o_cp("dve", t_["h"][:], ps[3][:], ["ps3"], ["e_h"])
                    self.o_act(t_["g"][:], t_["g"][:], AF.Ln, ["e_g"], ["e_g"])
                    self.o_act(t_["r0"][:], t_["g"][:], AF.Exp, ["e_g"], ["e_r0"], scale=-1.0)
                    self.o_act(t_["h"][:], t_["h"][:], AF.Ln, ["e_h"], ["e_h"])
                    self.o_act(t_["r1"][:], t_["h"][:], AF.Exp, ["e_h"], ["e_r1"], scale=-1.0)
                    self.o_tt("dve", t_["a"][:], t_["a"][:], t_["r0"][:], ALU.mult, ["e_a", "e_r0"], ["e_a"])
                    self.o_tt("dve", t_["b"][:], t_["b"][:], t_["r1"][:], ALU.mult, ["e_b", "e_r1"], ["e_b"])
                    self.o_stt(t_["c"][:], t_["b"][:], nlam[:], t_["a"][:], ALU.mult, ALU.add, ["e_a", "e_b", "nlam"], ["e_c"])
                    self.o_tt("dve", t_["d"][:], t_["c"][:], t_["c"][:], ALU.mult, ["e_c"], ["e_d"])

                    def tail(qsl=qsl):
                        self.o_mm(ps[1][:], self.ones_f[:], t_["d"][:], True, True, ["ones_f", "e_d"], ["ps1"])
                        self.o_act(t_["e"][:], ps[1][:], AF.Ln, ["ps1", "epsn"], ["e_e"], bias=epsn[:], scale=1.0 / 128.0)
                        self.o_act(t_["f"][:], t_["e"][:], AF.Exp, ["e_e"], ["e_f"], scale=-0.5)
                        self.o_stt(oT[:, qsl], t_["c"][:], gs[:], t_["f"][:], ALU.mult, ALU.mult, ["e_c", "e_f", "gs"], ["oT"])
                    pending.append(tail)
                while pending:
                    pending.pop(0)()
                P.dma("sp", "oT", self.OT[h], oT[:], reads=["oT"])
            self.end_phase()

    def phase_outproj(self, x_src, x_dst, w_d, src_fm, l):
        nc, P, ps = self.nc, self.P, self.ps
        NST = self.NST
        with ExitStack() as st:
            S = lambda name, shape, dtype: st.enter_context(nc.sbuf_tensor(f"{name}_p{self.nphase}", shape, dtype))
            A = {}
            wo = S("wo", [128, 8, D], BF16)
            ot = [S("ot0", [128, 8, 512], BF16), S("ot1", [128, 8, 512], BF16)]
            xres = [S("xres0", [128, D], F32), S("xres1", [128, D], F32)]
            self.alloc_ln(st, A, l, 1, nz=2)
            w_v = w_d.rearrange("(c p) f -> p c f", p=128)
            for g in range(2):
                P.dma("pool", f"wl{g}", wo[:, g * 4:(g + 1) * 4, :], w_v[:, g * 4:(g + 1) * 4, :], writes=[f"wo{g}"])
            cmul = 1.0 / ALPHA
            for t in range(NST):
                t0 = t * 512
                ts_ = t % 2
                P.dma("sp", f"ot{ts_}", ot[ts_][:], src_fm[:, :, t0:t0 + 512].rearrange("h p t -> p h t"),
                      writes=[f"ot{ts_}"])
                for s_ in range(4):
                    slot = s_ % 2
                    r0 = t0 + s_ * 128
                    if t == 0 and s_ == 0:
                        P.dma("sp", "xres0", xres[0][:], x_src[0:128, :], writes=["xres0"])
                    if r0 + 128 < self.T:
                        ns = 1 - slot
                        P.dma("sp", f"xres{ns}", xres[ns][:], x_src[r0 + 128:r0 + 256, :], writes=[f"xres{ns}"])
                    ya, yb = ps[4 + slot * 2], ps[5 + slot * 2]
                    yk = [f"ps{4 + slot * 2}", f"ps{5 + slot * 2}"]
                    for half, yp in enumerate((ya, yb)):
                        for h in range(8):
                            P.op("pe", lambda e, yp=yp, h=h, s_=s_, half=half, ts_=ts_: e.matmul(
                                yp[:], lhsT=ot[ts_][:, h, s_ * 128:(s_ + 1) * 128],
                                rhs=wo[:, h, half * 512:(half + 1) * 512], start=(h == 0), stop=(h == 7)),
                                reads=[f"ot{ts_}", f"wo{h // 4}"], writes=[yk[half]])
                    self.ln_epilogue_2bank(A, ya, yb, yk, xres[slot], f"xres{slot}", cmul, slot,
                                           x_dst[r0:r0 + 128, :], "st")
            self.end_phase()

    def o_mm(self, out, lhsT, rhs, start, stop, reads, writes):
        self.P.op("pe", lambda e: e.matmul(out, lhsT=lhsT, rhs=rhs, start=start, stop=stop), reads, writes)

    def o_tr(self, out, in_, ident, reads, writes):
        self.P.op("pe", lambda e: e.transpose(out=out, in_=in_, identity=ident), reads, writes)

    def o_tt(self, eng, out, in0, in1, op, reads, writes):
        self.P.op(eng, lambda e: e.tensor_tensor(out=out, in0=in0, in1=in1, op=op), reads, writes)

    def o_stt(self, out, in0, scalar, in1, op0, op1, reads, writes):
        self.P.op("dve", lambda e: e.scalar_tensor_tensor(out=out, in0=in0, scalar=scalar, in1=in1, op0=op0, op1=op1),
                  reads, writes)

    def o_ts(self, eng, out, in0, s1, op0, reads, writes):
        self.P.op(eng, lambda e: e.tensor_scalar(out=out, in0=in0, scalar1=s1, scalar2=None, op0=op0), reads, writes)

    def o_act(self, out, in_, func, reads, writes, bias=None, scale=None):
        kw = {}
        if bias is not None:
            kw["bias"] = bias
        if scale is not None:
            kw["scale"] = scale
        self.P.op("act", lambda e: e.activation(out=out, in_=in_, func=func, **kw), reads, writes)

    def o_cp(self, eng, out, in_, reads, writes):
        if eng == "act":
            self.P.op("act", lambda e: e.copy(out=out, in_=in_), reads, writes)
        else:
            self.P.op(eng, lambda e: e.tensor_copy(out=out, in_=in_), reads, writes)

    def o_rcp(self, out, in_, reads, writes):
        self.P.op("dve", lambda e: e.reciprocal(out=out, in_=in_), reads, writes)

    def o_rcpf(self, out, in_, reads, writes):
        self.P.op("dve", lambda e: e.reciprocal_approx_fast(out=out, in_=in_), reads, writes)

    def phase_gdn_proj(self, x_src, l):
        nc, P, ps = self.nc, self.P, self.ps
        NST = self.NST
        G = self.G
        with ExitStack() as st:
            S = lambda name, shape, dtype: st.enter_context(nc.sbuf_tensor(f"{name}_p{self.nphase}", shape, dtype))
            w = S("gw", [128, 8, 4112], BF16)
            xinb = S("xinb", [128, 4, D], BF16)
            xT = S("xT", [128, 8, 512], BF16)
            cwr = S("cwr", [24, 4, 128], F32)
            cw = S("cw", [128, 4, 24], F32)
            alog = S("alog", [8, 1], F32)
            dtb = S("dtb", [8, 1], F32)
            nexpA = S("nexpA", [8, 1], F32)
            one8 = S("one8", [8, 1], F32)
            eps6 = S("eps6", [128, 1], F32)
            sel = S("sel", [8, 8, 128], F32)
            scanm = S("scanm", [8, 512], F32)
            rows = {k: S("row_" + k, [8, 512], F32) for k in ("bl", "al", "gc", "egc", "ekd", "tmp")}
            pre = [S("pre0", [128, 515], F32), S("pre1", [128, 515], F32), S("pre2", [128, 515], F32)]
            halo = S("halo", [128, 24, 3], F32)
            cacc = [S("cacc0", [128, 512], F32), S("cacc1", [128, 512], F32)]
            qs = [S("qs0", [128, 512], F32), S("qs1", [128, 512], F32)]
            ks = [S("ks0", [128, 512], F32), S("ks1", [128, 512], F32)]
            vs = [S("vs0", [128, 512], F32), S("vs1", [128, 512], F32)]
            sqq = [S("sqq0", [128, 512], F32), S("sqq1", [128, 512], F32)]
            sqk = [S("sqk0", [128, 512], F32), S("sqk1", [128, 512], F32)]
            lnq = S("lnq", [128, 512], F32)
            lnk = S("lnk", [128, 512], F32)
            rsq = S("rsq", [128, 512], F32)
            rsk = S("rsk", [128, 512], F32)
            t1 = S("t1", [128, 512], F32)
            t2 = S("t2", [128, 512], F32)
            t3 = S("t3", [128, 512], F32)
            kbgT = [S("kbgT0", [128, 512], BF16), S("kbgT1", [128, 512], BF16)]
            kdT = [S("kdT0", [128, 512], BF16), S("kdT1", [128, 512], BF16)]
            vbT = [S("vbT0", [128, 512], BF16), S("vbT1", [128, 512], BF16)]
            stg = {nm: S("st_" + nm, [128, 4, 4, 128], BF16) for nm in ("gq", "gqg", "gkt", "gkb", "gkbg", "gkd", "gvb")}
            stg_sz = [S("st_gsz0", [128, 4, 4, 128], F32), S("st_gsz1", [128, 4, 4, 128], F32)]
            w_v = self.gdn_w_in[l].rearrange("(c p) f -> p c f", p=128)
            for kc in range(8):
                P.dma("pool", f"wl{kc}", w[:, kc, :], w_v[:, kc, :], writes=[f"gw{kc}"])
            gwk = [f"gw{kc}" for kc in range(8)]
            P.dma("sp", "cwr", cwr[:], self.gdn_conv_w[l].rearrange("j (c p) -> c j p", p=128), writes=["cwr"])
            P.dma("sp", "alog", alog[:], self.gdn_a_log[l].rearrange("(p o) -> p o", o=1), writes=["alog"])
            P.dma("sp", "dtb", dtb[:], self.gdn_dt_bias[l].rearrange("(p o) -> p o", o=1), writes=["dtb"])
            P.dma("sp", "sel", sel[:], self.consts_d[0:8, C_SEL:C_SEL + 1024].rearrange("k (h j) -> k h j", h=8),
                  writes=["sel"])
            P.dma("sp", "scanm", scanm[:], self.consts_d[0:8, C_SCAN:C_SCAN + 512], writes=["scanm"])
            P.op("dve", lambda e: e.memset(one8[:], 1.0), writes=["one8"])
            P.op("dve", lambda e: e.memset(eps6[:], 1e-6), writes=["eps6"])
            P.op("pool", lambda e: e.memset(halo[:], 0.0), writes=[f"halo{c}" for c in range(24)])
            for j in range(4):
                self.o_tr(ps[4][:, j * 24:(j + 1) * 24], cwr[:, j, :], self.ident_f[0:24, 0:24], ["cwr", "ident_f"], ["ps4"])
            self.o_cp("dve", cw[:], ps[4][:, 0:96].rearrange("p (j c) -> p j c", j=4), ["ps4"], ["cw"])
            self.o_act(nexpA[:], alog[:], AF.Exp, ["alog"], ["nexpA"])
            self.o_ts("dve", nexpA[:], nexpA[:], -1.0, ALU.mult, ["nexpA"], ["nexpA"])
            QSC = 128.0 ** -0.5
            nproj = 0
            for t in range(NST):
                t0 = t * 512
                self.emit_xT(x_src, t0, xinb, xT)
                xTk = [f"xT{kc}" for kc in range(8)]
                for kc in range(8):
                    self.o_mm(ps[4][0:8, :], w[:, kc, 4096:4104], xT[:, kc, :], kc == 0, kc == 7, [gwk[kc], xTk[kc]], ["ps4"])
                for kc in range(8):
                    self.o_mm(ps[5][0:8, :], w[:, kc, 4104:4112], xT[:, kc, :], kc == 0, kc == 7, [gwk[kc], xTk[kc]], ["ps5"])
                R_ = rows
                self.o_act(R_["bl"][:], ps[4][0:8, :], AF.Sigmoid, ["ps4"], ["r_bl"])
                self.o_act(R_["tmp"][:], ps[5][0:8, :], AF.Exp, ["ps5", "dtb"], ["r_tmp"], bias=dtb[:])
                self.o_act(R_["al"][:], R_["tmp"][:], AF.Ln, ["r_tmp", "one8"], ["r_al"], bias=one8[:])
                self.o_ts("dve", R_["al"][:], R_["al"][:], nexpA[:], ALU.mult, ["r_al", "nexpA"], ["r_al"])
                P.op("dve", lambda e: e.tensor_tensor_scan(out=R_["gc"][:], data0=scanm[:], data1=R_["al"][:], initial=0.0,
                                                           op0=ALU.mult, op1=ALU.add),
                     reads=["r_al", "scanm"], writes=["r_gc"])
                self.o_act(R_["egc"][:], R_["gc"][:], AF.Exp, ["r_gc"], ["r_egc"])
                for ch in range(4):
                    self.o_ts("dve", R_["tmp"][:, ch * 128:(ch + 1) * 128], R_["gc"][:, ch * 128:(ch + 1) * 128],
                              R_["gc"][:, ch * 128 + 127:ch * 128 + 128], ALU.subtract, ["r_gc"], ["r_tmp"])
                self.o_act(R_["ekd"][:], R_["tmp"][:], AF.Exp, ["r_tmp"], ["r_ekd"], scale=-1.0)
                P.dma("sp", "ggc", self.GGC[:, t0:t0 + 512], R_["gc"][:], reads=["r_gc"])
                v4 = lambda ap: ap.rearrange("p (n c) -> p n c", n=4)

                def stageA(h):
                    nonlocal nproj
                    hb = h % 2
                    hh = h % 4
                    for which in range(3):
                        cidx = which * 8 + h
                        b = 4 + nproj % 2
                        nproj += 1
                        bank = ps[b]
                        for kc in range(8):
                            self.o_mm(bank[:], w[:, kc, cidx * 128:(cidx + 1) * 128], xT[:, kc, :], kc == 0, kc == 7,
                                      [gwk[kc], xTk[kc]], [f"ps{b}"])
                        pr = pre[which]
                        pk = f"pre{which}"
                        self.o_cp("act", pr[:, 3:515], bank[:], [f"ps{b}"], [pk + "m"])
                        self.o_cp("pool", pr[:, 0:3], halo[:, cidx, :], [f"halo{cidx}"], [pk + "h"])
                        self.o_cp("pool", halo[:, cidx, :], pr[:, 512:515], [pk + "m"], [f"halo{cidx}"])
                    cidx = 24 + h
                    b = 4 + nproj % 2
                    nproj += 1
                    bank = ps[b]
                    for kc in range(8):
                        self.o_mm(bank[:], w[:, kc, cidx * 128:(cidx + 1) * 128], xT[:, kc, :], kc == 0, kc == 7,
                                  [gwk[kc], xTk[kc]], [f"ps{b}"])
                    self.o_act(stg_sz[h // 4][:, :, hh, :], bank[:].rearrange("p (n c) -> p n c", n=4), AF.Silu,
                               [f"ps{b}"], [f"st_gsz{h // 4}"])
                    for which, dstt, dk in ((0, qs[hb], f"qs{hb}"), (1, ks[hb], f"ks{hb}"), (2, vs[hb], f"vs{hb}")):
                        cidx = which * 8 + h
                        pr = pre[which]
                        pk = f"pre{which}"
                        ca = cacc[which % 2]
                        ck = f"cacc{which % 2}"
                        self.o_ts("dve", ca[:], pr[:, 0:512], cw[:, 0, cidx:cidx + 1], ALU.mult,
                                  [pk + "m", pk + "h", "cw"], [ck])
                        for j in range(1, 4):
                            self.o_stt(ca[:], pr[:, j:j + 512], cw[:, j, cidx:cidx + 1], ca[:], ALU.mult, ALU.add,
                                       [pk + "m", pk + "h", "cw", ck], [ck])
                        self.o_act(dstt[:], ca[:], AF.Silu, [ck], [dk])
                        if which == 0:
                            self.o_tt("pool", sqq[hb][:], qs[hb][:], qs[hb][:], ALU.mult, [f"qs{hb}"], [f"sqq{hb}"])
                        elif which == 1:
                            self.o_tt("pool", sqk[hb][:], ks[hb][:], ks[hb][:], ALU.mult, [f"ks{hb}"], [f"sqk{hb}"])

                def stageB1(h):
                    hb = h % 2
                    hh = h % 4
                    self.o_mm(ps[6][:], self.ones_f[:], sqq[hb][:], True, True, ["ones_f", f"sqq{hb}"], ["ps6"])
                    self.o_mm(ps[7][:], self.ones_f[:], sqk[hb][:], True, True, ["ones_f", f"sqk{hb}"], ["ps7"])
                    self.o_mm(ps[0][:], sel[:, h, :], R_["bl"][:], True, True, ["sel", "r_bl"], ["ps0"])
                    self.o_mm(ps[1][:], sel[:, h, :], R_["egc"][:], True, True, ["sel", "r_egc"], ["ps1"])
                    self.o_mm(ps[2][:], sel[:, h, :], R_["ekd"][:], True, True, ["sel", "r_ekd"], ["ps2"])
                    self.o_act(lnq[:], ps[6][:], AF.Ln, ["ps6", "eps6"], ["lnq"], bias=eps6[:])
                    self.o_act(rsq[:], lnq[:], AF.Exp, ["lnq"], ["rsq"], scale=-0.5)
                    self.o_act(lnk[:], ps[7][:], AF.Ln, ["ps7", "eps6"], ["lnk"], bias=eps6[:])
                    self.o_act(rsk[:], lnk[:], AF.Exp, ["lnk"], ["rsk"], scale=-0.5)
                    self.o_stt(t1[:], qs[hb][:], QSC, rsq[:], ALU.mult, ALU.mult, [f"qs{hb}", "rsq"], ["t1"])
                    self.o_cp("act", stg["gq"][:, :, hh, :], v4(t1[:]), ["t1"], ["st_gq"])
                    self.o_tt("dve", stg["gqg"][:, :, hh, :], v4(t1[:]), v4(ps[1][:]), ALU.mult, ["t1", "ps1"], ["st_gqg"])
                    self.o_tt("dve", t2[:], ks[hb][:], rsk[:], ALU.mult, [f"ks{hb}", "rsk"], ["t2"])
                    self.o_cp("act", stg["gkt"][:, :, hh, :], v4(t2[:]), ["t2"], ["st_gkt"])
                    self.o_tt("dve", t3[:], t2[:], ps[0][:], ALU.mult, ["t2", "ps0"], ["t3"])
                    self.o_cp("act", stg["gkb"][:, :, hh, :], v4(t3[:]), ["t3"], ["st_gkb"])
                    self.o_tt("dve", kbgT[hb][:], t3[:], ps[1][:], ALU.mult, ["t3", "ps1"], [f"kbgT{hb}"])
                    self.o_tt("dve", kdT[hb][:], t2[:], ps[2][:], ALU.mult, ["t2", "ps2"], [f"kdT{hb}"])
                    self.o_tt("dve", vbT[hb][:], vs[hb][:], ps[0][:], ALU.mult, [f"vs{hb}", "ps0"], [f"vbT{hb}"])
                    if hh == 3:
                        hg = h // 4
                        for nm in ("gq", "gqg", "gkt", "gkb"):
                            P.dma("sp", "st_" + nm,
                                  G[nm][4 * t:4 * t + 4, :, hg * 512:(hg + 1) * 512].rearrange("n p f -> p n f"),
                                  stg[nm][:].rearrange("p n h c -> p n (h c)"), reads=["st_" + nm])
                        P.dma("sp", f"st_gsz{hg}",
                              G["gsz"][4 * t:4 * t + 4, :, hg * 512:(hg + 1) * 512].rearrange("n p f -> p n f"),
                              stg_sz[hg][:].rearrange("p n h c -> p n (h c)"), reads=[f"st_gsz{hg}"])

                def stageB2(h):
                    hb = h % 2
                    hh = h % 4
                    pb = ps[3].bitcast(BF16)
                    for (src, sk, nm, ev) in ((kbgT[hb], f"kbgT{hb}", "gkbg", "act"), (kdT[hb], f"kdT{hb}", "gkd", "dve"),
                                              (vbT[hb], f"vbT{hb}", "gvb", "act")):
                        for ch in range(4):
                            self.o_tr(pb[:, ch * 128:(ch + 1) * 128], src[:, ch * 128:(ch + 1) * 128], self.ident_b[:],
                                      [sk, "ident_b"], ["ps3"])
                        self.o_cp(ev, stg[nm][:, :, hh, :], pb[:, 0:512].rearrange("p (n c) -> p n c", n=4),
                                  ["ps3"], ["st_" + nm])
                    if hh == 3:
                        hg = h // 4
                        for nm in ("gkbg", "gkd", "gvb"):
                            P.dma("sp", "st_" + nm,
                                  G[nm][4 * t:4 * t + 4, :, hg * 512:(hg + 1) * 512].rearrange("n p f -> p n f"),
                                  stg[nm][:].rearrange("p n h c -> p n (h c)"), reads=["st_" + nm])

                stageA(0)
                for h in range(8):
                    if h + 1 < 8:
                        stageA(h + 1)
                    stageB1(h)
                    if h >= 1:
                        stageB2(h - 1)
                stageB2(7)
            self.end_phase()

    def phase_gdn_core(self, l):
        nc, P, ps = self.nc, self.P, self.ps
        T = self.T
        NCH = T // 128
        G = self.G
        arrs = ("gq", "gqg", "gkt", "gkb", "gkbg", "gkd", "gvb")
        with ExitStack() as st:
            S = lambda name, shape, dtype: st.enter_context(nc.sbuf_tensor(f"{name}_p{self.nphase}", shape, dtype))
            ld = [{nm: S(f"ld{sl}_{nm}", [128, 8, 128], BF16) for nm in arrs} for sl in range(2)]
            ldsz = [S(f"ld{sl}_gsz", [128, 8, 128], F32) for sl in range(2)]
            gcr = [S(f"gcr{sl}", [8, 128], F32) for sl in range(2)]
            triadd = S("triadd", [128, 4, 128], F32)
            strict = S("strict", [128, 4, 128], F32)
            ident4 = S("ident4", [128, 4, 128], F32)
            sel = S("sel", [8, 8, 128], F32)
            seln = S("seln", [8, 8, 128], F32)
            i8 = S("i8", [8, 8], F32)
            glI = S("glI", [8, 8], F32)
            cd = S("cd", [128, 8], F32)
            gn = S("gn", [128, 1], F32)
            epsn = S("epsn", [128, 1], F32)
            Sf = S("Sf", [128, 8, 128], F32)
            Sb = S("Sb", [128, 8, 128], BF16)
            ogst = [S("ogst0", [128, 8, 512], BF16), S("ogst1", [128, 8, 512], BF16)]
            W = []
            for g in range(2):
                d = {}
                for nm in ("tmpD", "E", "Es", "Na", "Nb", "Ma", "Mb", "Ua", "Ub2", "sq", "rstd", "tt"):
                    d[nm] = S(f"g{g}_{nm}", [128, 4, 128], F32)
                for nm in ("attnT", "Ubf", "nwT", "vn"):
                    d[nm] = S(f"g{g}_{nm}", [128, 4, 128], BF16)
                W.append(d)
            cdr = self.consts_d
            P.dma("sp", "c_tri", triadd[:], cdr[:, C_TRIADD4:C_TRIADD4 + 512].rearrange("p (h c) -> p h c", h=4), writes=["triadd"])
            P.dma("sp", "c_str", strict[:], cdr[:, C_STRICT4:C_STRICT4 + 512].rearrange("p (h c) -> p h c", h=4), writes=["strict"])
            P.dma("sp", "c_id4", ident4[:], cdr[:, C_IDENT4:C_IDENT4 + 512].rearrange("p (h c) -> p h c", h=4), writes=["ident4"])
            P.dma("sp", "c_sel", sel[:], cdr[0:8, C_SEL:C_SEL + 1024].rearrange("k (h j) -> k h j", h=8), writes=["sel"])
            P.dma("sp", "c_i8", i8[:], cdr[0:8, C_I8:C_I8 + 8], writes=["i8"])
            P.dma("sp", "c_gn", gn[:], self.gdn_norm_g[l].rearrange("(p o) -> p o", o=1), writes=["gn"])
            self.o_ts("dve", seln[:], sel[:], -1.0, ALU.mult, ["sel"], ["seln"])
            P.op("dve", lambda e: e.memset(epsn[:], 1e-5), writes=["epsn"])
            P.op("dve", lambda e: e.memset(Sf[:], 0.0), writes=["Sf0", "Sf1"])
            P.op("pool", lambda e: e.memset(Sb[:], 0.0), writes=["Sb0", "Sb1"])
            v4 = lambda bank: bank[:].rearrange("p (h c) -> p h c", h=4)

            def load(n):
                sl = n % 2
                for nm in arrs:
                    P.dma("sp", f"ld{sl}_{nm}", ld[sl][nm][:].rearrange("p h c -> p (h c)"), G[nm][n], writes=[f"ld{sl}_{nm}"])
                P.dma("sp", f"ld{sl}_gsz", ldsz[sl][:].rearrange("p h c -> p (h c)"), G["gsz"][n], writes=[f"ld{sl}_gsz"])
                P.dma("sp", f"gcr{sl}", gcr[sl][:], self.GGC[:, n * 128:(n + 1) * 128], writes=[f"gcr{sl}"])

            load(0)
            for n in range(NCH):
                sl = n % 2
                if n + 1 < NCH:
                    load(n + 1)
                L = ld[sl]
                lk = lambda nm: f"ld{sl}_{nm}"
                gk = f"gcr{sl}"
                Bk = [[f"ps{g * 4 + i}" for i in range(4)] for g in range(2)]
                Bn = [[ps[g * 4 + i] for i in range(4)] for g in range(2)]
                self.o_ts("dve", glI[:], i8[:], gcr[sl][:, 127:128], ALU.mult, ["i8", gk], ["glI"])
                self.o_mm(ps[2][:, 0:8], self.ones_f[0:8, :], glI[:], True, True, ["ones_f", "glI"], ["ps2"])
                self.o_act(cd[:], ps[2][:, 0:8], AF.Exp, ["ps2"], ["cd"])
                for g in range(2):
                    for hh in range(4):
                        h = g * 4 + hh
                        o_ = Bn[g][0][:, hh * 128:(hh + 1) * 128]
                        self.o_mm(o_, sel[:, h, :], gcr[sl][:], True, False, ["sel", gk], [Bk[g][0]])
                        self.o_mm(o_, gcr[sl][:], seln[:, h, :], False, True, ["seln", gk], [Bk[g][0]])
                for g in range(2):
                    w_ = W[g]
                    self.o_tt("dve", w_["tmpD"][:], v4(Bn[g][0]), triadd[:], ALU.add, [Bk[g][0], "triadd"], [f"g{g}tmpD"])
                    self.o_act(w_["E"][:], w_["tmpD"][:], AF.Exp, [f"g{g}tmpD"], [f"g{g}E"])
                    self.o_tt("pool", w_["Es"][:], w_["E"][:], strict[:], ALU.mult, [f"g{g}E", "strict"], [f"g{g}Es"])
                for g in range(2):
                    for hh in range(4):
                        h = g * 4 + hh
                        self.o_mm(Bn[g][1][:, hh * 128:(hh + 1) * 128], L["gkt"][:, h, :], L["gkb"][:, h, :], True, True,
                                  [lk("gkt"), lk("gkb")], [Bk[g][1]])
                    for hh in range(4):
                        h = g * 4 + hh
                        self.o_mm(Bn[g][2][:, hh * 128:(hh + 1) * 128], L["gkt"][:, h, :], L["gq"][:, h, :], True, True,
                                  [lk("gkt"), lk("gq")], [Bk[g][2]])
                for g in range(2):
                    w_ = W[g]
                    self.o_stt(w_["Na"][:], v4(Bn[g][1]), -1.0, w_["Es"][:], ALU.mult, ALU.mult, [Bk[g][1], f"g{g}Es"], [f"g{g}Na"])
                    self.o_tt("dve", w_["attnT"][:], v4(Bn[g][2]), w_["E"][:], ALU.mult, [Bk[g][2], f"g{g}E"], [f"g{g}attnT"])
                for g in range(2):
                    w_ = W[g]
                    for hh in range(4):
                        self.o_tr(Bn[g][3][:, hh * 128:(hh + 1) * 128], w_["Na"][:, hh, :], self.ident_f[:],
                                  [f"g{g}Na", "ident_f"], [Bk[g][3]])
                    self.o_cp("act", w_["Ma"][:], v4(Bn[g][3]), [Bk[g][3]], [f"g{g}Ma"])
                    self.o_tt("dve", w_["Ua"][:], w_["Na"][:], ident4[:], ALU.add, [f"g{g}Na", "ident4"], [f"g{g}Ua"])
                cur = ["a", "a"]
                for lev in range(1, 7):
                    for g in range(2):
                        w_ = W[g]
                        c_ = cur[g]
                        nx = "b" if c_ == "a" else "a"
                        Np, Mp = w_["N" + c_], w_["M" + c_]
                        Nn, Mn = w_["N" + nx], w_["M" + nx]
                        kNp, kMp = f"g{g}N{c_}", f"g{g}M{c_}"
                        kNn, kMn = f"g{g}N{nx}", f"g{g}M{nx}"
                        if lev < 6:
                            for hh in range(4):
                                self.o_mm(Bn[g][0][:, hh * 128:(hh + 1) * 128], Mp[:, hh, :], Np[:, hh, :], True, True,
                                          [kNp, kMp], [Bk[g][0]])
                            self.o_cp("act", Nn[:], v4(Bn[g][0]), [Bk[g][0]], [kNn])
                        else:
                            for hh in range(4):
                                self.o_mm(Bn[g][1][:, hh * 128:(hh + 1) * 128], Np[:, hh, :], Mp[:, hh, :], True, True,
                                          [kNp, kMp], [Bk[g][1]])
                            self.o_cp("dve", Mn[:], v4(Bn[g][1]), [Bk[g][1]], [kMn])
                    if lev < 6:
                        for g in range(2):
                            w_ = W[g]
                            c_ = cur[g]
                            nx = "b" if c_ == "a" else "a"
                            Nn, Mn = w_["N" + nx], w_["M" + nx]
                            kNn, kMn = f"g{g}N{nx}", f"g{g}M{nx}"
                            for hh in range(4):
                                self.o_tr(Bn[g][1][:, hh * 128:(hh + 1) * 128], Nn[:, hh, :], self.ident_f[:],
                                          [kNn, "ident_f"], [Bk[g][1]])
                            self.o_cp("dve", Mn[:], v4(Bn[g][1]), [Bk[g][1]], [kMn])
                    for g in range(2):
                        w_ = W[g]
                        c_ = cur[g]
                        nx = "b" if c_ == "a" else "a"
                        Mn = w_["M" + nx]
                        kMn = f"g{g}M{nx}"
                        Up = w_["Ua"] if c_ == "a" else w_["Ub2"]
                        Un = w_["Ub2"] if c_ == "a" else w_["Ua"]
                        kUp = f"g{g}U{c_}"
                        kUn = f"g{g}U{nx}"
                        for hh in range(4):
                            self.o_mm(Bn[g][2][:, hh * 128:(hh + 1) * 128], Mn[:, hh, :], Up[:, hh, :], True, True,
                                      [kMn, kUp], [Bk[g][2]])
                        self.o_tt("dve", Un[:], Up[:], v4(Bn[g][2]), ALU.add, [kUp, Bk[g][2]], [kUn])
                        cur[g] = nx
                for g in range(2):
                    w_ = W[g]
                    Uf = w_["Ua"] if cur[g] == "a" else w_["Ub2"]
                    kU = f"g{g}U{cur[g]}"
                    self.o_cp("act", w_["Ubf"][:], Uf[:], [kU], [f"g{g}Ubf"])
                    for hh in range(4):
                        h = g * 4 + hh
                        self.o_mm(Bn[g][3][:, hh * 128:(hh + 1) * 128], L["gkbg"][:, h, :], w_["Ubf"][:, hh, :], True, True,
                                  [lk("gkbg"), f"g{g}Ubf"], [Bk[g][3]])
                    P.op("act", lambda e, w_=w_, bank=Bn[g][3]: e.mul(out=w_["nwT"][:], in_=v4(bank), mul=-1.0),
                         reads=[Bk[g][3]], writes=[f"g{g}nwT"])
                for g in range(2):
                    w_ = W[g]
                    for hh in range(4):
                        h = g * 4 + hh
                        o_ = Bn[g][0][:, hh * 128:(hh + 1) * 128]
                        self.o_mm(o_, w_["Ubf"][:, hh, :], L["gvb"][:, h, :], True, False, [f"g{g}Ubf", lk("gvb")], [Bk[g][0]])
                        self.o_mm(o_, w_["nwT"][:, hh, :], Sb[:, h, :], False, True, [f"g{g}nwT", f"Sb{g}"], [Bk[g][0]])
                    self.o_cp("act", w_["vn"][:], v4(Bn[g][0]), [Bk[g][0]], [f"g{g}vn"])
                for g in range(2):
                    w_ = W[g]
                    for hh in range(4):
                        h = g * 4 + hh
                        o_ = Bn[g][1][:, hh * 128:(hh + 1) * 128]
                        self.o_mm(o_, Sb[:, h, :], L["gqg"][:, h, :], True, False, [f"Sb{g}", lk("gqg")], [Bk[g][1]])
                        self.o_mm(o_, w_["vn"][:, hh, :], w_["attnT"][:, hh, :], False, True, [f"g{g}vn", f"g{g}attnT"], [Bk[g][1]])
                    for hh in range(4):
                        h = g * 4 + hh
                        self.o_mm(Bn[g][2][:, hh * 128:(hh + 1) * 128], L["gkd"][:, h, :], w_["vn"][:, hh, :], True, True,
                                  [lk("gkd"), f"g{g}vn"], [Bk[g][2]])
                for g in range(2):
                    w_ = W[g]
                    for hh in range(4):
                        h = g * 4 + hh
                        self.o_stt(Sf[:, h, :], Sf[:, h, :], cd[:, h:h + 1], Bn[g][2][:, hh * 128:(hh + 1) * 128],
                                   ALU.mult, ALU.add, [f"Sf{g}", "cd", Bk[g][2]], [f"Sf{g}"])
                    self.o_cp("act", Sb[:, g * 4:(g + 1) * 4, :], Sf[:, g * 4:(g + 1) * 4, :], [f"Sf{g}"], [f"Sb{g}"])
                for g in range(2):
                    w_ = W[g]
                    self.o_act(w_["sq"][:], v4(Bn[g][1]), AF.Square, [Bk[g][1]], [f"g{g}sq"])
                    self.o_mm(Bn[g][3][:], self.ones_f[:], w_["sq"][:].rearrange("p h c -> p (h c)"), True, True,
                              ["ones_f", f"g{g}sq"], [Bk[g][3]])
                    self.o_act(w_["rstd"][:], v4(Bn[g][3]), AF.Ln, [Bk[g][3], "epsn"], [f"g{g}rstd"], bias=epsn[:], scale=1.0 / 128.0)
                    self.o_act(w_["sq"][:], w_["rstd"][:], AF.Exp, [f"g{g}rstd"], [f"g{g}sq"], scale=-0.5)
                    self.o_stt(w_["tt"][:], v4(Bn[g][1]), gn[:], w_["sq"][:], ALU.mult, ALU.mult,
                               [Bk[g][1], "gn", f"g{g}sq"], [f"g{g}tt"])
                    osl = (n // 4) % 2
                    c4 = n % 4
                    self.o_tt("pool", ogst[osl][:, g * 4:(g + 1) * 4, c4 * 128:(c4 + 1) * 128], w_["tt"][:],
                              ldsz[sl][:, g * 4:(g + 1) * 4, :], ALU.mult, [f"g{g}tt", f"ld{sl}_gsz"], [f"ogst{osl}"])
                if n % 4 == 3:
                    osl = (n // 4) % 2
                    t0 = (n // 4) * 512
                    P.dma("sp", f"ogst{osl}", self.OT[:, :, t0:t0 + 512].rearrange("h p t -> p h t"), ogst[osl][:],
                          reads=[f"ogst{osl}"])
            self.end_phase()

    def phase_copy_out(self, x_src):
        P = self.P
        T = self.T
        n = max(1, T // 2048)
        rows = T // n
        for i in range(n):
            P.dma("sp", "fin", self.y_out[i * rows:(i + 1) * rows, :], x_src[i * rows:(i + 1) * rows, :])
        self.end_phase()

    def build(self):
        self.phase_setup()
        plan = self.plan if self.plan is not None else full_plan()
        cur = self.x_in
        bufs = [self.xa, self.xb]
        nb = 0
        for step in plan:
            kind = step[0]
            if kind == "ffn":
                _, which, l = step
                w_in = (self.ffn1_w_in if which == 1 else self.ffn2_w_in)[l]
                w_out = (self.ffn1_w_out if which == 1 else self.ffn2_w_out)[l]
                dst = bufs[nb]
                nb ^= 1
                self.phase_ffn(cur, dst, w_in, w_out, l, 0 if which == 1 else 2)
                cur = dst
            elif kind == "kv":
                self.phase_proj_fm(cur, self.diff_w_kv, 0, 8, self.KT, None, also_v=1024)
            elif kind == "attn":
                _, l = step
                jl = l - N_A
                lambda_init = 0.8 - 0.6 * math.exp(-0.3 * l)
                self.phase_proj_fm(cur, self.diff_w_q[jl], 0, 8, self.QT, 0.125)
                self.phase_attn(jl, lambda_init)
                dst = bufs[nb]
                nb ^= 1
                self.phase_outproj(cur, dst, self.diff_w_out[jl], self.OT, l)
                cur = dst
            elif kind == "gdn":
                _, l = step
                self.phase_gdn_proj(cur, l)
                self.phase_gdn_core(l)
                dst = bufs[nb]
                nb ^= 1
                self.phase_outproj(cur, dst, self.gdn_w_out[l], self.OT, l)
                cur = dst
        self.phase_copy_out(cur)
        return self.nc


def full_plan():
    plan = []
    for l in range(DEPTH):
        plan.append(("ffn", 1, l))
        plan.append(("gdn", l) if l < N_A else ("attn", l))
        plan.append(("ffn", 2, l))
        if l == N_A - 1:
            plan.append(("kv",))
    return plan


_CACHE = {}


def get_program(T, plan=None, debug=False):
    key = (T, repr(plan), debug)
    if key not in _CACHE:
        _CACHE[key] = Builder(T, None, debug=debug, plan=plan).build()
    return _CACHE[key]


WEIGHT_NAMES = ["ln_g", "ln_b", "ffn1_w_in", "ffn1_w_out", "ffn2_w_in", "ffn2_w_out", "gdn_w_in",
                "gdn_conv_w", "gdn_a_log", "gdn_dt_bias", "gdn_norm_g", "gdn_w_out", "diff_w_kv",
                "diff_lambda_k", "diff_w_q", "diff_lambda_q", "diff_norm_g", "diff_w_out"]


def run(inputs, plan=None, trace=False, debug=False):
    x = np.ascontiguousarray(np.asarray(inputs["x"], dtype=np.float32))
    B, T, _ = x.shape
    assert B == NCORES
    nc = get_program(T, plan, debug)
    consts = make_consts()
    shared = {k: np.ascontiguousarray(np.asarray(inputs[k], dtype=np.float32)) for k in WEIGHT_NAMES}
    in_maps = []
    for b in range(B):
        m = {"x": x[b], "consts": consts}
        m.update(shared)
        in_maps.append(m)
    res = run_bass_kernel_spmd(nc, in_maps, core_ids=list(range(NCORES)), trace=trace)
    out = np.stack([res.results[b]["y"] for b in range(B)], axis=0)
    return out, res


def kernel(**inputs):
    out, _ = run(inputs)
    return out
```
